# Optimizing a Trainium2 kernel written in Bass

```python
import math
import jax, jax.numpy as jnp
from jax import lax
import numpy as np

D_MODEL = 2048
BATCH = 4
SEQ = 4096
DEPTH = 4

CHUNK = 64
Q_BLOCK = 128
ROPE_THETA = 10000.0
EPS = 1e-6

DIFF_HEADS = 8
DIFF_QK_DIM = 64
DIFF_V_DIM = 128

MLA_HEADS = 8
MLA_Q_RANK = 512
MLA_KV_RANK = 256
MLA_NOPE_DIM = 128
MLA_ROPE_DIM = 64
MLA_V_DIM = 128

MEM_TOKENS = 256
MEM_HEADS = 4
MEM_HEAD_DIM = D_MODEL // MEM_HEADS

D_FF = 5632
CONV_WIDTH = 3

DIFF_WIDTH = DIFF_HEADS * DIFF_V_DIM
MLA_WIDTH = MLA_HEADS * MLA_V_DIM
IN_WIDTHS = (DIFF_HEADS * 2 * DIFF_QK_DIM,
             DIFF_HEADS * 2 * DIFF_QK_DIM,
             DIFF_WIDTH,
             MLA_Q_RANK,
             MLA_KV_RANK,
             MLA_ROPE_DIM,
             2 * D_MODEL)
N_IN = sum(IN_WIDTHS)

kernel_name = "hybrid_diffattn_mla_gated_convffn"


def _rms(x, g):
    xf = x.astype(jnp.float32)
    y = xf * lax.rsqrt(jnp.mean(xf * xf, axis=-1, keepdims=True) + EPS)
    return (y * g.astype(jnp.float32)).astype(x.dtype)


def _rope(x, pos):
    d = x.shape[-1]
    inv = ROPE_THETA ** (-jnp.arange(0, d, 2, dtype=jnp.float32) / d)
    ang = pos.astype(jnp.float32)[..., None] * inv
    cos = jnp.cos(ang)[:, :, None, :]
    sin = jnp.sin(ang)[:, :, None, :]
    xf = x.astype(jnp.float32)
    x1, x2 = xf[..., : d // 2], xf[..., d // 2:]
    return jnp.concatenate([x1 * cos - x2 * sin, x2 * cos + x1 * sin], axis=-1).astype(x.dtype)


def _chunk_mask(blk, seq):
    q_chunk = (blk * Q_BLOCK + jnp.arange(Q_BLOCK)) // CHUNK
    k_chunk = jnp.arange(seq) // CHUNK
    return k_chunk[None, :] <= q_chunk[:, None]


def _masked_softmax(s, mask):
    return jax.nn.softmax(jnp.where(mask, s.astype(jnp.float32), -jnp.inf), axis=-1)


def _sweep(fn, *qs):
    b, s = qs[0].shape[:2]
    nb = s // Q_BLOCK
    blocks = tuple(q.reshape(b, nb, Q_BLOCK, *q.shape[2:]).swapaxes(0, 1) for q in qs)
    out = lax.map(lambda a: fn(a[0], *a[1:]), (jnp.arange(nb), *blocks))
    return out.swapaxes(0, 1).reshape(b, s, *out.shape[3:])


def _diff_attention(dq, dk, dv, lam, lam_init, g_sub, pos):
    b, s, _ = dq.shape
    q = dq.reshape(b, s, DIFF_HEADS, 2 * DIFF_QK_DIM)
    k = dk.reshape(b, s, DIFF_HEADS, 2 * DIFF_QK_DIM)
    v = dv.reshape(b, s, DIFF_HEADS, DIFF_V_DIM)
    q1 = _rope(q[..., :DIFF_QK_DIM], pos)
    q2 = _rope(q[..., DIFF_QK_DIM:], pos)
    k1 = _rope(k[..., :DIFF_QK_DIM], pos)
    k2 = _rope(k[..., DIFF_QK_DIM:], pos)
    scale = 1.0 / math.sqrt(DIFF_QK_DIM)

    def block(blk, q1b, q2b):
        mask = _chunk_mask(blk, s)
        p1 = _masked_softmax(jnp.einsum('bqhd,bkhd->bhqk', q1b, k1) * scale, mask)
        p2 = _masked_softmax(jnp.einsum('bqhd,bkhd->bhqk', q2b, k2) * scale, mask)
        p = (p1 - lam * p2).astype(v.dtype)
        return jnp.einsum('bhqk,bkhd->bqhd', p, v)

    o = _sweep(block, q1, q2)
    o = _rms(o, g_sub) * (1.0 - lam_init)
    return o.reshape(b, s, DIFF_WIDTH)


def _mla(c_q, c_kv, k_rope, g_cq, w_uq, g_ckv, w_ukv, pos):
    b, s, _ = c_q.shape
    q = (_rms(c_q, g_cq) @ w_uq).reshape(b, s, MLA_HEADS, MLA_NOPE_DIM + MLA_ROPE_DIM)
    q_nope = q[..., :MLA_NOPE_DIM]
    q_pe = _rope(q[..., MLA_NOPE_DIM:], pos)
    kv = (_rms(c_kv, g_ckv) @ w_ukv).reshape(b, s, MLA_HEADS, MLA_NOPE_DIM + MLA_V_DIM)
    k_nope = kv[..., :MLA_NOPE_DIM]
    v = kv[..., MLA_NOPE_DIM:]
    k_pe = _rope(k_rope[:, :, None, :], pos)[:, :, 0, :]
    scale = 1.0 / math.sqrt(MLA_NOPE_DIM + MLA_ROPE_DIM)

    def block(blk, qnb, qpb):
        mask = _chunk_mask(blk, s)
        sc = jnp.einsum('bqhd,bkhd->bhqk', qnb, k_nope) + jnp.einsum('bqhd,bkd->bhqk', qpb, k_pe)
        p = _masked_softmax(sc * scale, mask).astype(v.dtype)
        return jnp.einsum('bhqk,bkhd->bqhd', p, v)

    return _sweep(block, q_nope, q_pe).reshape(b, s, MLA_WIDTH)


def _memory_attention(h, m, w_q, w_kv, w_o):
    b, s, _ = h.shape
    q = (h @ w_q).reshape(b, s, MEM_HEADS, MEM_HEAD_DIM)
    kv = (m @ w_kv).reshape(b, m.shape[1], 2, MEM_HEADS, MEM_HEAD_DIM)
    k, v = kv[:, :, 0], kv[:, :, 1]
    sc = jnp.einsum('bqhd,bkhd->bhqk', q, k) * (1.0 / math.sqrt(MEM_HEAD_DIM))
    p = jax.nn.softmax(sc.astype(jnp.float32), axis=-1).astype(v.dtype)
    o = jnp.einsum('bhqk,bkhd->bqhd', p, v).reshape(b, s, D_MODEL)
    return o @ w_o


def _conv_glu(h, w_up, conv_w, conv_b, w_down):
    s = h.shape[1]
    u = h @ w_up
    up = jnp.pad(u, ((0, 0), (CONV_WIDTH - 1, 0), (0, 0)))
    c = sum(up[:, j:j + s, :] * conv_w[j] for j in range(CONV_WIDTH)) + conv_b
    a, g = c[..., :D_FF], c[..., D_FF:]
    return (jax.nn.silu(a) * g) @ w_down


def setup_inputs(seed: int = 0) -> dict:
    key = jax.random.key(seed)
    ks = iter(jax.random.split(key, 40))
    L = DEPTH

    def w(shape, fan_in):
        return jax.random.normal(next(ks), shape, jnp.float32) * fan_in ** -0.5

    def gain(shape):
        return 1.0 + 0.05 * jax.random.normal(next(ks), shape, jnp.float32)

    x = jax.random.normal(next(ks), (BATCH, SEQ, D_MODEL), jnp.float32)
    mem = jax.random.normal(next(ks), (BATCH, MEM_TOKENS, D_MODEL), jnp.float32)
    offset = jax.random.randint(next(ks), (BATCH, 1), 0, 64, dtype=jnp.int32) * CHUNK
    positions = (offset + jnp.arange(SEQ, dtype=jnp.int32)[None, :]).astype(jnp.int32)
    return {
        "x": x,
        "mem": mem,
        "positions": positions,
        "g_pre_mix": gain((L, D_MODEL)),
        "w_in": w((L, D_MODEL, N_IN), D_MODEL),
        "b_gate": 0.1 * jax.random.normal(next(ks), (L, 2 * D_MODEL), jnp.float32),
        "lam_q1": 0.1 * jax.random.normal(next(ks), (L, DIFF_QK_DIM), jnp.float32),
        "lam_k1": 0.1 * jax.random.normal(next(ks), (L, DIFF_QK_DIM), jnp.float32),
        "lam_q2": 0.1 * jax.random.normal(next(ks), (L, DIFF_QK_DIM), jnp.float32),
        "lam_k2": 0.1 * jax.random.normal(next(ks), (L, DIFF_QK_DIM), jnp.float32),
        "g_diff_sub": gain((L, DIFF_V_DIM)),
        "g_cq": gain((L, MLA_Q_RANK)),
        "w_uq": w((L, MLA_Q_RANK, MLA_HEADS * (MLA_NOPE_DIM + MLA_ROPE_DIM)), MLA_Q_RANK),
        "g_ckv": gain((L, MLA_KV_RANK)),
        "w_ukv": w((L, MLA_KV_RANK, MLA_HEADS * (MLA_NOPE_DIM + MLA_V_DIM)), MLA_KV_RANK),
        "w_br_diff": w((L, DIFF_WIDTH, D_MODEL), DIFF_WIDTH),
        "w_br_mla": w((L, MLA_WIDTH, D_MODEL), MLA_WIDTH),
        "w_mix_out": w((L, D_MODEL, D_MODEL), D_MODEL),
        "g_post_mix": gain((L, D_MODEL)),
        "g_pre_x": gain((L, D_MODEL)),
        "g_mem": gain((L, D_MODEL)),
        "w_q_x": w((L, D_MODEL, D_MODEL), D_MODEL),
        "w_kv_x": w((L, D_MODEL, 2 * D_MODEL), D_MODEL),
        "w_o_x": w((L, D_MODEL, D_MODEL), D_MODEL),
        "g_post_x": gain((L, D_MODEL)),
        "g_pre_ffn": gain((L, D_MODEL)),
        "w_up": w((L, D_MODEL, 2 * D_FF), D_MODEL),
        "conv_w": w((L, CONV_WIDTH, 2 * D_FF), CONV_WIDTH),
        "conv_b": 0.02 * jax.random.normal(next(ks), (L, 2 * D_FF), jnp.float32),
        "w_down": w((L, D_FF, D_MODEL), D_FF),
        "g_post_ffn": gain((L, D_MODEL)),
    }


def reference(x, mem, positions, g_pre_mix, w_in, b_gate, lam_q1, lam_k1, lam_q2, lam_k2,
              g_diff_sub, g_cq, w_uq, g_ckv, w_ukv, w_br_diff, w_br_mla, w_mix_out,
              g_post_mix, g_pre_x, g_mem, w_q_x, w_kv_x, w_o_x, g_post_x, g_pre_ffn,
              w_up, conv_w, conv_b, w_down, g_post_ffn):
    split_at = [int(i) for i in np.cumsum(IN_WIDTHS)[:-1]]
    for l in range(DEPTH):
        h = _rms(x, g_pre_mix[l])
        proj = h @ w_in[l]
        dq, dk, dv, c_q, c_kv, k_rope, gt = jnp.split(proj, split_at, axis=-1)

        lam_init = 0.8 - 0.6 * math.exp(-0.3 * l)
        lam = (jnp.exp(jnp.sum(lam_q1[l].astype(jnp.float32) * lam_k1[l].astype(jnp.float32)))
               - jnp.exp(jnp.sum(lam_q2[l].astype(jnp.float32) * lam_k2[l].astype(jnp.float32)))
               + lam_init)
        o_diff = _diff_attention(dq, dk, dv, lam, lam_init, g_diff_sub[l], positions)
        o_mla = _mla(c_q, c_kv, k_rope, g_cq[l], w_uq[l], g_ckv[l], w_ukv[l], positions)

        gates = jax.nn.sigmoid((gt + b_gate[l]).astype(jnp.float32)).astype(x.dtype)
        g_a, g_b = gates[..., :D_MODEL], gates[..., D_MODEL:]
        merged = g_a * (o_diff @ w_br_diff[l]) + g_b * (o_mla @ w_br_mla[l])
        x = x + _rms(merged @ w_mix_out[l], g_post_mix[l])

        h = _rms(x, g_pre_x[l])
        m = _rms(mem, g_mem[l])
        x = x + _rms(_memory_attention(h, m, w_q_x[l], w_kv_x[l], w_o_x[l]), g_post_x[l])

        h = _rms(x, g_pre_ffn[l])
        x = x + _rms(_conv_glu(h, w_up[l], conv_w[l], conv_b[l], w_down[l]), g_post_ffn[l])
    return x
```

```python
import math
import contextlib
import numpy as np
import concourse.bass as bass
import concourse.mybir as mybir
from concourse.bass_utils import run_bass_kernel_spmd

F32 = mybir.dt.float32
BF16 = mybir.dt.bfloat16
I32 = mybir.dt.int32
AF = mybir.ActivationFunctionType
ALU = mybir.AluOpType
AX = mybir.AxisListType

L = 4
D = 2048
T = 2048
NT = 4
TW = 512
NB = 16
DC = 16
EPS = 1e-6
NIN = 8064
NVL = 503
V_GPM, V_GPOM, V_GPX, V_GPOX, V_GPF, V_GPOF, V_GMEM, V_BG, V_GCQ, V_GCKV, V_GSUB, V_CW0, V_CW1, V_CW2, V_CB = \
    0, 16, 32, 48, 64, 80, 96, 112, 144, 148, 150, 151, 239, 327, 415
NEG = -30000.0
FFN_SPLITS = [(0, 12), (12, 24), (24, 34), (34, 44)]
PAIRS = [[0, 1], [2, 3], [4, 5], [6, 7]]


class Res:
    __slots__ = ("name", "w_ops", "w_dma", "r_ops", "r_dma")

    def __init__(self, name):
        self.name = name
        self.w_ops = {}
        self.w_dma = {}
        self.r_ops = {}
        self.r_dma = {}


ENGS = ("pe", "act", "dve", "pool", "sp")


class Sched:
    def __init__(self):
        self.items = {e: [] for e in ENGS}
        self.nops = {e: 0 for e in ENGS}
        self.signal = {e: set() for e in ENGS}
        self.known_op = {e: {f: 0 for f in ENGS} for e in ENGS}
        self.known_dma = {e: {} for e in ENGS}
        self.dma_cnt = {}
        self.dma_sems = []

    def _waits(self, eng, reads, writes, partial=False):
        need_op = {}
        need_dma = {}
        for r in reads:
            for f, i in r.w_ops.items():
                if i > need_op.get(f, 0):
                    need_op[f] = i
            for s, c in r.w_dma.items():
                if c > need_dma.get(s, 0):
                    need_dma[s] = c
        for r in writes:
            for d in ((r.r_ops,) if partial else (r.w_ops, r.r_ops)):
                for f, i in d.items():
                    if i > need_op.get(f, 0):
                        need_op[f] = i
            for d in ((r.r_dma,) if partial else (r.w_dma, r.r_dma)):
                for s, c in d.items():
                    if c > need_dma.get(s, 0):
                        need_dma[s] = c
        for f, i in need_op.items():
            if f == eng and eng == "pe":
                continue
            if i > self.known_op[eng][f]:
                self.known_op[eng][f] = i
                self.signal[f].add(i)
                self.items[eng].append(("wait_op", f, i))
        for s, c in need_dma.items():
            if s not in ("cc",):
                c = self.dma_cnt[s]
            if c > self.known_dma[eng].get(s, 0):
                self.known_dma[eng][s] = c
                self.items[eng].append(("wait_dma", s, c))

    def op(self, eng, fn, reads=(), writes=()):
        self._waits(eng, reads, writes)
        self.nops[eng] += 1
        idx = self.nops[eng]
        self.items[eng].append(("op", fn, idx))
        for r in reads:
            r.r_ops[eng] = idx
        for r in writes:
            r.w_ops = {eng: idx}
            r.w_dma = {}
            r.r_ops = {}
            r.r_dma = {}
        return idx

    def dma(self, eng, fn, sem, reads=(), writes=(), inc=16, partial=False):
        self._waits(eng, reads, writes, partial)
        c = self.dma_cnt.get(sem, 0) + inc
        self.dma_cnt[sem] = c
        if sem not in self.dma_sems:
            self.dma_sems.append(sem)
        self.items[eng].append(("dma", fn, sem, inc))
        for r in reads:
            r.r_dma[sem] = c
        for r in writes:
            if partial:
                r.w_dma[sem] = c
            else:
                r.w_ops = {}
                r.w_dma = {sem: c}
                r.r_ops = {}
                r.r_dma = {}

    def emit(self, block, sems_eng, sems_dma):
        sched = self
        counts = {}
        for e in ENGS:
            counts[e] = {idx: k + 1 for k, idx in enumerate(sorted(sched.signal[e]))}

        def run(eng, h):
            for it in sched.items[eng]:
                k = it[0]
                if k == "wait_op":
                    h.wait_ge(sems_eng[it[1]], counts[it[1]][it[2]])
                elif k == "wait_dma":
                    h.wait_ge(sems_dma[it[1]], it[2])
                elif k == "op":
                    ins = it[1](h)
                    if it[2] in counts[eng]:
                        ins.then_inc(sems_eng[eng], 1)
                else:
                    ins = it[1](h)
                    ins.then_inc(sems_dma[it[2]], it[3])

        @block.tensor
        def _(h):
            run("pe", h)

        @block.scalar
        def _(h):
            run("act", h)

        @block.vector
        def _(h):
            run("dve", h)

        @block.gpsimd
        def _(h):
            run("pool", h)

        @block.sync
        def _(h):
            run("sp", h)


def MM(out, lhsT, rhs, st, sp):
    return lambda h: h.matmul(out, lhsT, rhs, start=st, stop=sp)


def TR(out, in_, ident):
    return lambda h: h.transpose(out, in_, ident)


def ACT(out, in_, func, bias=None, scale=None):
    kw = {}
    if bias is not None:
        kw["bias"] = bias
    if scale is not None:
        kw["scale"] = scale
    return lambda h: h.activation(out=out, in_=in_, func=func, **kw)


def TS(out, in0, s1, s2, op0, op1=None):
    if op1 is None:
        return lambda h: h.tensor_scalar(out=out, in0=in0, scalar1=s1, scalar2=None, op0=op0)
    return lambda h: h.tensor_scalar(out=out, in0=in0, scalar1=s1, scalar2=s2, op0=op0, op1=op1)


def STT(out, in0, scalar, in1, op0, op1):
    return lambda h: h.scalar_tensor_tensor(out=out, in0=in0, scalar=scalar, in1=in1, op0=op0, op1=op1)


def TT(out, in0, in1, op):
    return lambda h: h.tensor_tensor(out=out, in0=in0, in1=in1, op=op)


def CP(out, in_):
    return lambda h: h.tensor_copy(out=out, in_=in_)


def RCP(out, in_):
    return lambda h: h.reciprocal(out=out, in_=in_)


def DMA(out, in_):
    return lambda h: h.dma_start(out=out, in_=in_)


def CC(in_ap, out_ap):
    return lambda h: h.collective_compute("AllGather", ALU.bypass, replica_groups=PAIRS, ins=[in_ap], outs=[out_ap])


_DBG = {"on": False, "data": None}


def build_program():
    nc = bass.Bass("TRN2", target_bir_lowering=False)
    S = Sched()
    DBG = _DBG["on"]
    L_RUN = 1 if DBG else L

    def din(name, shape, dt=F32):
        return nc.dram_tensor(name, shape, dt, kind="ExternalInput").ap()

    def dscr(name, shape, dt):
        return nc.dram_tensor(name, shape, dt, kind="Internal").ap()

    x_in = din("x", [T, D])
    mem_in = din("mem", [256, D])
    pos_in = din("pos", [1, T], I32)
    vecs_in = din("vecs", [128, L * NVL])
    lam_in = din("lamv", [1, L * 256])
    cst_in = din("cst", [128, 8])
    mask_in = din("masks", [128, 8 * TW])
    mats_in = din("mats", [128, 3 * 128])
    w_in = din("w_in", [L, D, NIN])
    w_uq = din("w_uq", [L, 512, 1536])
    w_ukv = din("w_ukv", [L, 256, 2048])
    w_brd = din("w_br_diff", [L, 1024, D])
    w_brm = din("w_br_mla", [L, 1024, D])
    w_mo = din("w_mix_out", [L, D, D])
    w_qx = din("w_q_x", [L, D, D])
    w_kvx = din("w_kv_x", [L, D, 2 * D])
    w_ox = din("w_o_x", [L, D, D])
    w_up = din("w_up", [L, D, 11264])
    w_dn = din("w_down", [L, 5632, D])
    out_d = nc.dram_tensor("out", [T, D], F32, kind="ExternalOutput").ap()

    dbg_d = nc.dram_tensor("dbg", [3, 128, DC, T], F32, kind="ExternalOutput").ap() if DBG else None
    R_dbg = Res("dbg")
    dbgA_d = nc.dram_tensor("dbgA", [2, 128, DC * T], BF16, kind="ExternalOutput").ap() if DBG else None
    dbg_k = [0]
    xT_d = dscr("xT_d", [128, DC, T], F32)
    yT_d = dscr("yT_d", [128, DC, T], F32)
    qd_d = dscr("qd_d", [8, 128, T], BF16)
    qmn_d = dscr("qmn_d", [8, 128, T], BF16)
    qmp_d = dscr("qmp_d", [4, 128, T], BF16)
    gates_d = dscr("gates_d", [32, 128, T], BF16)
    exin_d = dscr("exin_d", [16, 256, T], BF16)
    exout_d = dscr("exout_d", [16, 512, T], BF16)
    kpe_in_d = dscr("kpe_in_d", [128, T], BF16)
    kpe_out_d = dscr("kpe_out_d", [256, T], BF16)
    halo_in_d = dscr("halo_in_d", [128, 512], BF16)
    halo_out_d = dscr("halo_out_d", [256, 512], BF16)
    rope_d = dscr("rope_d", [2, 128, T], F32)
    memn_d = dscr("memn_d", [128, DC, 256], F32)
    kmT_d = dscr("kmT_d", [L, 128, DC, 256], BF16)
    vm_d = dscr("vm_d", [L, 128, 2, D], BF16)

    R_xT = [Res(f"xT{t}") for t in range(NT)]
    R_yT = [[Res(f"yT{c}_{t}") for t in range(NT)] for c in range(DC)]
    R_qd = [Res(f"qd{h}") for h in range(8)]
    R_qmn = [Res(f"qmn{h}") for h in range(8)]
    R_qmp = [Res(f"qmp{h}") for h in range(4)]
    R_gates = [Res(f"gates{c}") for c in range(32)]
    R_exin = [Res(f"exin{h}") for h in range(16)]
    R_exout = [Res(f"exout{h}") for h in range(16)]
    R_kpein, R_kpeout = Res("kpein"), Res("kpeout")
    R_haloin, R_haloout = Res("haloin"), Res("haloout")
    R_rope = Res("rope")
    R_memn = Res("memn")
    R_kmT = [Res(f"kmT{l}") for l in range(L)]
    R_vm = [Res(f"vm{l}") for l in range(L)]
    R_out = Res("out")

    with contextlib.ExitStack() as es:
        def sb(name, shape, dt):
            return es.enter_context(nc.sbuf_tensor(name, shape, dt))

        A = sb("bigA", [128, DC * T], BF16)
        B32 = sb("bigB", [128, 16384], F32)
        A3 = A[:].rearrange("p (c t) -> p c t", t=T)
        Bb = B32[:].bitcast(BF16)
        Bb3 = Bb.rearrange("p (c t) -> p c t", t=T)
        R_A = [[Res(f"A{c}_{t}") for t in range(NT)] for c in range(DC)]
        R_B = [[Res(f"B{c}_{t}") for t in range(NT)] for c in range(DC)]

        def RA_tt(t):
            return [R_A[c][t] for c in range(DC)]

        def RA_all():
            return [R_A[c][t] for c in range(DC) for t in range(NT)]

        R_Bmisc = []

        def RB_all():
            return [R_B[c][t] for c in range(DC) for t in range(NT)] + R_Bmisc

        WB = [sb(f"wb{i}", [128, 6144], BF16) for i in range(2)]
        R_WB = [Res(f"wb{i}") for i in range(2)]
        vecs = sb("vecs_sb", [128, L * NVL], F32)
        lamt = B32[:, 4096:4096 + L * 256]
        lamw = sb("lamw", [128, 16], F32)
        gsubs = sb("gsubs", [128, L], F32)
        cst = sb("cst_sb", [128, 8], F32)
        masks = Bb[:, 28672:32768]
        R_masks = Res("masks")
        mats = sb("mats_sb", [128, 3 * 128], F32)
        matsb = sb("matsb_sb", [128, 128], BF16)
        ident = mats[:, 0:128]
        perm = mats[:, 128:256]
        ones_bf = matsb[:, :]
        R_const = Res("const")
        R_lamt = Res("lamt")
        R_Bmisc.append(R_lamt)
        NWK = 8
        wk = [sb(f"wk{i}", [128, 520], F32) for i in range(NWK)]
        R_wk = [Res(f"wk{i}") for i in range(NWK)]
        NST = 2
        stg = [sb(f"stg{i}", [128, T], BF16) for i in range(NST)]
        R_stg = [Res(f"stg{i}") for i in range(NST)]
        NPB = 6
        pb = [sb(f"pb{i}", [128, TW], BF16) for i in range(NPB)]
        R_pb = [Res(f"pb{i}") for i in range(NPB)]
        hh = sb("hh_sb", [128, 512], BF16)
        hc = sb("hc_sb", [128, 512], BF16)
        hg = sb("hg_sb", [128, 1024], BF16)
        R_hh, R_hc, R_hg = Res("hh"), Res("hc"), Res("hg")
        posi = sb("posi", [128, TW], I32)
        R_posi = Res("posi")
        ps = [es.enter_context(nc.psum_tensor(f"ps{i}", [128, TW], F32)) for i in range(8)]
        R_ps = [Res(f"ps{i}") for i in range(8)]

        rot = {"wk": 0, "stg": 0, "pb": 0, "wb": 0}

        def nxt(kind, n):
            i = rot[kind]
            rot[kind] = (i + 1) % n
            return i

        def V(l, off, j=0):
            c = l * NVL + off + j
            return vecs[:, c:c + 1]

        S.dma("sp", DMA(vecs[:], vecs_in[:, :]), "ld_const", writes=[R_const])
        S.dma("sp", DMA(cst[:], cst_in[:, :]), "ld_const", writes=[R_const])
        S.dma("sp", DMA(mats[:], mats_in[:, :]), "ld_const", writes=[R_const])
        S.dma("sp", DMA(lamt, bass.AP(lam_in.tensor, 0, [[0, 128], [1, L * 256]])), "ld_const", writes=[R_const, R_lamt])
        S.op("dve", CP(matsb[:, :], mats[:, 256:384]), reads=[R_const], writes=[R_const])
        for l in range(L):
            lam_init = 0.8 - 0.6 * math.exp(-0.3 * l)
            b = l * 256
            S.op("dve", TT(lamt[:, b:b + 64], lamt[:, b:b + 64], lamt[:, b + 64:b + 128], ALU.mult), reads=[R_const, R_lamt], writes=[R_const])
            S.op("dve", TT(lamt[:, b + 128:b + 192], lamt[:, b + 128:b + 192], lamt[:, b + 192:b + 256], ALU.mult), reads=[R_const, R_lamt], writes=[R_const])
            S.op("dve", lambda h, b=b, l=l: h.reduce_sum(out=lamw[:, 4 * l:4 * l + 1], in_=lamt[:, b:b + 64], axis=AX.X), reads=[R_const, R_lamt], writes=[R_const])
            S.op("dve", lambda h, b=b, l=l: h.reduce_sum(out=lamw[:, 4 * l + 1:4 * l + 2], in_=lamt[:, b + 128:b + 192], axis=AX.X), reads=[R_const, R_lamt], writes=[R_const])
            S.op("act", ACT(lamw[:, 4 * l:4 * l + 2], lamw[:, 4 * l:4 * l + 2], AF.Exp), reads=[R_const], writes=[R_const])
            S.op("dve", TT(lamw[:, 4 * l + 2:4 * l + 3], lamw[:, 4 * l + 1:4 * l + 2], lamw[:, 4 * l:4 * l + 1], ALU.subtract), reads=[R_const], writes=[R_const])
            S.op("dve", TS(lamw[:, 4 * l + 3:4 * l + 4], lamw[:, 4 * l + 2:4 * l + 3], -lam_init, None, ALU.add), reads=[R_const], writes=[R_const])
            S.op("dve", TS(gsubs[:, l:l + 1], V(l, V_GSUB), 1.0 - lam_init, None, ALU.mult), reads=[R_const], writes=[R_const])

        def neglam(l):
            return lamw[:, 4 * l + 3:4 * l + 4]

        C1, C2 = 6.28125, 2.0 * math.pi - 6.28125
        MAGIC = 12582912.0

        def wrap_pi(u, t_):
            S.op("dve", TS(wk[t_][:, 0:TW], wk[u][:, 0:TW], math.pi, -2.0 * math.pi, ALU.is_gt, ALU.mult), reads=[R_wk[u]], writes=[R_wk[t_]])
            S.op("dve", TT(wk[u][:, 0:TW], wk[u][:, 0:TW], wk[t_][:, 0:TW], ALU.add), reads=[R_wk[u], R_wk[t_]], writes=[R_wk[u]])
            S.op("dve", TS(wk[t_][:, 0:TW], wk[u][:, 0:TW], -math.pi, 2.0 * math.pi, ALU.is_lt, ALU.mult), reads=[R_wk[u]], writes=[R_wk[t_]])
            S.op("dve", TT(wk[u][:, 0:TW], wk[u][:, 0:TW], wk[t_][:, 0:TW], ALU.add), reads=[R_wk[u], R_wk[t_]], writes=[R_wk[u]])

        for t in range(NT):
            S.dma("sp", DMA(posi[:], bass.AP(pos_in.tensor, t * TW, [[0, 128], [1, TW]])), "ld_posi", writes=[R_posi])
            a0, a1, a2 = 0, 1, 2
            S.op("dve", CP(wk[a0][:, 0:TW], posi[:]), reads=[R_posi], writes=[R_wk[a0]])
            S.op("dve", TS(wk[a0][:, 0:TW], wk[a0][:, 0:TW], cst[:, 0:1], None, ALU.mult), reads=[R_wk[a0], R_const], writes=[R_wk[a0]])
            S.op("dve", TS(wk[a1][:, 0:TW], wk[a0][:, 0:TW], 1.0 / (2.0 * math.pi), MAGIC, ALU.mult, ALU.add), reads=[R_wk[a0]], writes=[R_wk[a1]])
            S.op("dve", TS(wk[a1][:, 0:TW], wk[a1][:, 0:TW], -MAGIC, None, ALU.add), reads=[R_wk[a1]], writes=[R_wk[a1]])
            S.op("dve", STT(wk[a0][:, 0:TW], wk[a1][:, 0:TW], -C1, wk[a0][:, 0:TW], ALU.mult, ALU.add), reads=[R_wk[a0], R_wk[a1]], writes=[R_wk[a0]])
            S.op("dve", STT(wk[a0][:, 0:TW], wk[a1][:, 0:TW], -C2, wk[a0][:, 0:TW], ALU.mult, ALU.add), reads=[R_wk[a0], R_wk[a1]], writes=[R_wk[a0]])
            wrap_pi(a0, a1)
            S.op("act", ACT(wk[a2][:, 0:TW], wk[a0][:, 0:TW], AF.Sin), reads=[R_wk[a0]], writes=[R_wk[a2]])
            S.op("dve", TS(wk[a2][:, 0:TW], wk[a2][:, 0:TW], cst[:, 1:2], None, ALU.mult), reads=[R_wk[a2], R_const], writes=[R_wk[a2]])
            S.dma("sp", DMA(rope_d[1, :, t * TW:(t + 1) * TW], wk[a2][:, 0:TW]), "st_wk2", reads=[R_wk[a2]], writes=[R_rope], partial=True)
            S.op("dve", TS(wk[a0][:, 0:TW], wk[a0][:, 0:TW], 0.5 * math.pi, None, ALU.add), reads=[R_wk[a0]], writes=[R_wk[a0]])
            wrap_pi(a0, a1)
            S.op("act", ACT(wk[3][:, 0:TW], wk[a0][:, 0:TW], AF.Sin), reads=[R_wk[a0]], writes=[R_wk[3]])
            S.dma("sp", DMA(rope_d[0, :, t * TW:(t + 1) * TW], wk[3][:, 0:TW]), "st_wk3", reads=[R_wk[3]], writes=[R_rope], partial=True)

        def rstd_from_bank(bank_i, n, outw):
            S.op("act", ACT(wk[outw][:, 0:TW], ps[bank_i][:], AF.Ln, bias=cst[:, 4:5], scale=1.0 / n), reads=[R_ps[bank_i], R_const], writes=[R_wk[outw]])
            S.op("act", ACT(wk[outw][:, 0:TW], wk[outw][:, 0:TW], AF.Exp, scale=-0.5), reads=[R_wk[outw]], writes=[R_wk[outw]])

        def load_w(W2d, KC, col0, ncols, row0=0, slot_kc0=0, buf=None, tot_kc=None):
            i = nxt("wb", 2) if buf is None else buf
            tot = KC if tot_kc is None else tot_kc
            assert tot * ncols <= 6144
            wv = WB[i][:, 0:tot * ncols].rearrange("p (kc n) -> p kc n", n=ncols)
            src = W2d[row0:row0 + KC * 128, col0:col0 + ncols].rearrange("(kc p) n -> p kc n", p=128)
            S.dma("pool", DMA(wv[:, slot_kc0:slot_kc0 + KC, :], src), f"ld_wb{i}", writes=[R_WB[i]], partial=(slot_kc0 > 0))
            return i, wv

        def gemm_fm(W2d, KC, chunk0, nchunks, act_ap, act_res, consume, banks, row0=0, tt_outer=False, tts=range(NT), head_tt=False):
            deferred = []
            bi = [0]
            nmax = min(4, 6144 // (KC * 128))
            blocks = []
            c = 0
            while c < nchunks:
                nblk = min(nmax, nchunks - c)
                if tt_outer and nchunks == 4:
                    nblk = 2
                blocks.append((c, nblk))
                c += nblk

            def run_tile(c0, oc, tt, i, wv):
                nonlocal deferred
                b = banks[bi[0] % len(banks)]
                bi[0] += 1
                rd = [R_WB[i]] + act_res(tt)
                for kc in range(KC):
                    first, last = kc == 0, kc == KC - 1
                    S.op("pe", MM(ps[b][:], wv[:, kc, oc * 128:(oc + 1) * 128], act_ap(kc, tt), first, last),
                         reads=rd if (first or last) else (), writes=[R_ps[b]])
                prev = deferred
                deferred = []
                for f in prev:
                    f()
                consume(c0 + oc, tt, b, deferred)

            if tt_outer:
                assert len(blocks) <= 2
                loaded = [(c0, nblk) + load_w(W2d, KC, (chunk0 + c0) * 128, nblk * 128, row0=row0) for (c0, nblk) in blocks]
                for tt in tts:
                    for (c0, nblk, i, wv) in loaded:
                        for oc in range(nblk):
                            run_tile(c0, oc, tt, i, wv)
            else:
                rest = blocks
                if head_tt:
                    head, rest = blocks[:2], blocks[2:]
                    loaded = [(c0, nblk) + load_w(W2d, KC, (chunk0 + c0) * 128, nblk * 128, row0=row0) for (c0, nblk) in head]
                    for tt in tts:
                        for (c0, nblk, i, wv) in loaded:
                            for oc in range(nblk):
                                run_tile(c0, oc, tt, i, wv)
                for (c0, nblk) in rest:
                    i, wv = load_w(W2d, KC, (chunk0 + c0) * 128, nblk * 128, row0=row0)
                    for oc in range(nblk):
                        for tt in tts:
                            run_tile(c0, oc, tt, i, wv)
            for f in deferred:
                f()

        def gemm_tm(W2d, KC, col0, ncols, act_blk_ap, act_res_blk, consume, banks, row0=0):
            i, wv = load_w(W2d, KC, col0, ncols, row0=row0)
            for blk in range(NB):
                b = banks[blk % len(banks)]
                rd = [R_WB[i]] + act_res_blk(blk)
                for kc in range(KC):
                    first, last = kc == 0, kc == KC - 1
                    S.op("pe", MM(ps[b][:, 0:ncols], act_blk_ap(kc, blk), wv[:, kc, :], first, last),
                         reads=rd if (first or last) else (), writes=[R_ps[b]])
                consume(blk, b)

        hT_ap = lambda kc, tt: A3[:, kc, tt * TW:(tt + 1) * TW]
        hT_res = lambda tt: RA_tt(tt)

        xn = B32[:, 0:8192].rearrange("p (c t) -> p c t", t=TW)
        yt = B32[:, 8192:16384].rearrange("p (c t) -> p c t", t=TW)
        xin = B32[:, 8192:16384].rearrange("p (b d) -> p b d", d=D)
        R_xn = [[Res(f"xn{c}") for c in range(DC)]]
        R_yt = [Res(f"yt{c}") for c in range(DC)]
        R_Bmisc.extend(R_xn[0] + R_yt)

        def nr_pass(l, mode, g_post_off, g_pre_l, g_pre_off):
            RB = RB_all()
            for tt in range(NT):
                tsl = slice(tt * TW, (tt + 1) * TW)
                if mode == "init":
                    for blk in range(4):
                        r0 = tt * TW + blk * 128
                        S.dma("sp", DMA(xin[:, blk, :], x_in[r0:r0 + 128, :]), "ld_yt", writes=[R_yt[blk]] + (RB if (tt == 0 and blk == 0) else []))
                    for c in range(DC):
                        b = c % 4
                        for blk in range(4):
                            S.op("pe", TR(ps[b][:, blk * 128:(blk + 1) * 128], xin[:, blk, c * 128:(c + 1) * 128], ident),
                                 reads=[R_yt[blk], R_const], writes=[R_ps[b]])
                        S.op("act", ACT(xn[:, c, :], ps[b][:], AF.Copy), reads=[R_ps[b]], writes=[R_xn[0][c]] + (RB if (tt == 0 and c == 0) else []))
                else:
                    for c in range(DC):
                        S.dma("sp", DMA(yt[:, c, :], yT_d[:, c, tsl]), "ld_yt", reads=[R_yT[c][tt]], writes=[R_yt[c]] + (RB if (tt == 0 and c == 0) else []))
                    for c in range(DC):
                        p = nxt("pb", NPB)
                        S.op("act", ACT(pb[p][:], yt[:, c, :], AF.Square), reads=[R_yt[c]], writes=[R_pb[p]])
                        S.op("pe", MM(ps[4][:], ones_bf, pb[p][:], c == 0, c == DC - 1), reads=[R_pb[p], R_const], writes=[R_ps[4]])
                    rstd_from_bank(4, D, 6)
                    for c in range(DC):
                        w = nxt("wk", 4)
                        S.dma("sp", DMA(wk[w][:, 0:TW], xT_d[:, c, tsl]), f"ld_wk{w}", reads=[R_xT[tt]], writes=[R_wk[w]])
                        S.op("dve", STT(yt[:, c, :], yt[:, c, :], V(l, g_post_off, c), wk[6][:, 0:TW], ALU.mult, ALU.mult),
                             reads=[R_yt[c], R_wk[6], R_const], writes=[R_yt[c]])
                        S.op("dve", TT(xn[:, c, :], yt[:, c, :], wk[w][:, 0:TW], ALU.add), reads=[R_yt[c], R_wk[w]],
                             writes=[R_xn[0][c]] + (RB if (tt == 0 and c == 0) else []))
                if mode != "final":
                    for c in range(DC):
                        S.dma("sp", DMA(xT_d[:, c, tsl], xn[:, c, :]), "st_xn", reads=[R_xn[0][c]], writes=[R_xT[tt]], partial=(c > 0))
                        if DBG and mode == "mid":
                            S.dma("sp", DMA(dbg_d[dbg_k[0], :, c, tsl], xn[:, c, :]), "st_xn", reads=[R_xn[0][c]], writes=[R_dbg], partial=True)
                        p = nxt("pb", NPB)
                        S.op("act", ACT(pb[p][:], xn[:, c, :], AF.Square), reads=[R_xn[0][c]], writes=[R_pb[p]])
                        S.op("pe", MM(ps[5][:], ones_bf, pb[p][:], c == 0, c == DC - 1), reads=[R_pb[p], R_const], writes=[R_ps[5]])
                    rstd_from_bank(5, D, 7)
                    for c in range(DC):
                        S.op("dve", STT(A3[:, c, tsl], xn[:, c, :], V(g_pre_l, g_pre_off, c), wk[7][:, 0:TW], ALU.mult, ALU.mult),
                             reads=[R_xn[0][c], R_wk[7], R_const], writes=[R_A[c][tt]])
                    if DBG and mode == "mid" and tt == NT - 1:
                        dbg_k[0] += 1
                else:
                    for blk in range(4):
                        ot = yt
                        otile = B32[:, 8192 + (blk % 2) * 2048: 8192 + (blk % 2) * 2048 + 2048]
                        for c in range(DC):
                            b = c // 4
                            S.op("pe", TR(ps[b][:, (c % 4) * 128:(c % 4 + 1) * 128], xn[:, c, blk * 128:(blk + 1) * 128], ident),
                                 reads=[R_xn[0][c], R_const], writes=[R_ps[b]])
                        for b in range(4):
                            S.op("act" if b % 2 == 0 else "dve", (ACT(otile[:, b * 512:(b + 1) * 512], ps[b][:], AF.Copy) if b % 2 == 0 else CP(otile[:, b * 512:(b + 1) * 512], ps[b][:])),
                                 reads=[R_ps[b]], writes=[R_yt[(blk % 2) * 4 + b]])
                        r0 = tt * TW + blk * 128
                        S.dma("sp", DMA(out_d[r0:r0 + 128, :], otile), "st_out", reads=[R_yt[(blk % 2) * 4 + b] for b in range(4)], writes=[R_out], partial=True)

        def consume_store_bf16(dst_fn, res_fn, func=AF.Copy, bias_fn=None):
            seen = set()

            def consume(oc, tt, b, deferred):
                p = nxt("pb", NPB)
                kw = {} if bias_fn is None else {"bias": bias_fn(oc)}
                S.op("act", ACT(pb[p][:], ps[b][:], func, **kw), reads=[R_ps[b], R_const], writes=[R_pb[p]])
                S.dma("sp", DMA(dst_fn(oc)[:, tt * TW:(tt + 1) * TW], pb[p][:]), f"st_pb{p}", reads=[R_pb[p]], writes=[res_fn(oc)], partial=(oc in seen))
                seen.add(oc)
            return consume

        ropeC = B32[:, 0:2048]
        ropeS = B32[:, 2048:4096]
        R_ropes = Res("ropes")
        R_Bmisc.append(R_ropes)

        def consume_rope(dst_fn, res_fn):
            seen = set()

            def consume(oc, tt, b, deferred):
                q = nxt("wk", 4)
                S.op("act", ACT(wk[q][:, 0:TW], ps[b][:], AF.Copy), reads=[R_ps[b]], writes=[R_wk[q]])

                def later():
                    pbk = 6 + (q % 2)
                    S.op("pe", MM(ps[pbk][:], perm, wk[q][:, 0:TW], True, True), reads=[R_wk[q], R_const], writes=[R_ps[pbk]])
                    t2 = 4 + (q % 2)
                    S.op("dve", TT(wk[t2][:, 0:TW], ps[pbk][:], ropeS[:, tt * TW:(tt + 1) * TW], ALU.mult), reads=[R_ps[pbk], R_ropes], writes=[R_wk[t2]])
                    S.op("dve", TT(wk[q][:, 0:TW], wk[q][:, 0:TW], ropeC[:, tt * TW:(tt + 1) * TW], ALU.mult), reads=[R_wk[q], R_ropes], writes=[R_wk[q]])
                    p = nxt("pb", NPB)
                    S.op("dve", TT(pb[p][:], wk[q][:, 0:TW], wk[t2][:, 0:TW], ALU.add), reads=[R_wk[q], R_wk[t2]], writes=[R_pb[p]])
                    S.dma("sp", DMA(dst_fn(oc)[:, tt * TW:(tt + 1) * TW], pb[p][:]), f"st_pb{p}", reads=[R_pb[p]], writes=[res_fn(oc)], partial=(oc in seen))
                    seen.add(oc)
                deferred.append(later)
            return consume

        def consume_tokmajor_V(h0, nh):
            def consume(blk, b):
                p = nxt("pb", NPB)
                S.op("act", ACT(pb[p][:, 0:nh * 128], ps[b][:, 0:nh * 128], AF.Copy), reads=[R_ps[b]], writes=[R_pb[p]])
                dst = exin_d[h0:h0 + nh, 128:256, blk * 128:(blk + 1) * 128].rearrange("h p d -> p h d")
                S.dma("sp", DMA(dst, pb[p][:, 0:nh * 128].rearrange("p (h d) -> p h d", d=128)), f"st_pb{p}", reads=[R_pb[p]],
                      writes=[R_exin[h0 + j] for j in range(nh)], partial=True)
            return consume

        cqn = Bb[:, 8192:8192 + 4 * T].rearrange("p (c t) -> p c t", t=T)
        ckvn = Bb[:, 8192 + 4 * T:8192 + 6 * T].rearrange("p (c t) -> p c t", t=T)
        R_cqn = [Res(f"cqn{t}") for t in range(NT)]
        R_ckvn = [Res(f"ckvn{t}") for t in range(NT)]
        R_Bmisc.extend(R_cqn + R_ckvn)

        def consume_latent(l, nck, dstv, dres, goff, bank_ssq):
            def consume(oc, tt, b, deferred):
                w = oc
                S.op("act", ACT(wk[w][:, 0:TW], ps[b][:], AF.Copy), reads=[R_ps[b]], writes=[R_wk[w]])
                p = nxt("pb", NPB)
                S.op("act", ACT(pb[p][:], wk[w][:, 0:TW], AF.Square), reads=[R_wk[w]], writes=[R_pb[p]])

                def later():
                    S.op("pe", MM(ps[bank_ssq][:], ones_bf, pb[p][:], oc == 0, oc == nck - 1), reads=[R_pb[p], R_const], writes=[R_ps[bank_ssq]])
                    if oc == nck - 1:
                        rstd_from_bank(bank_ssq, nck * 128, 5)
                        for c2 in range(nck):
                            S.op("dve", STT(dstv[:, c2, tt * TW:(tt + 1) * TW], wk[c2][:, 0:TW], V(l, goff, c2), wk[5][:, 0:TW], ALU.mult, ALU.mult),
                                 reads=[R_wk[c2], R_wk[5], R_const], writes=[dres[tt]])
                deferred.append(later)
            return consume

        def attn_keytiles(tt):
            tiles = []
            for r in range(2):
                for kb in range(4 * tt + 4):
                    tiles.append((r, kb, (r * 4 + kb - 4 * tt) if kb >= 4 * tt else None))
            return tiles

        def hb(slot):
            base = slot * 12288
            q = Bb[:, base:base + 2048]
            q2 = Bb[:, base + 2048:base + 4096]
            k = Bb[:, base + 4096:base + 8192].rearrange("p (r t) -> p r t", t=T)
            v = Bb[:, base + 8192:base + 12288].rearrange("p (s d) -> p s d", d=128)
            return q, q2, k, v
        kpe_sb = Bb[:, 24576:24576 + 4096].rearrange("p (r t) -> p r t", t=T)
        R_hb = [Res("hb0"), Res("hb1")]
        R_kpe = Res("kpe_sb")
        R_Bmisc.extend(R_hb + [R_kpe, R_masks])

        def load_head(slot, h, kind):
            q, q2, k, v = hb(slot)
            if kind == "diff":
                S.dma("sp", DMA(q, qd_d[h]), f"ld_hb{slot}", reads=[R_qd[h]], writes=[R_hb[slot]])
                e = h
            else:
                S.dma("sp", DMA(q, qmn_d[h]), f"ld_hb{slot}", reads=[R_qmn[h]], writes=[R_hb[slot]])
                S.dma("sp", DMA(q2, qmp_d[h // 2]), f"ld_hb{slot}", reads=[R_qmp[h // 2]], writes=[R_hb[slot]], partial=True)
                e = 8 + h
            for r in range(2):
                S.dma("sp", DMA(k[:, r, :], exout_d[e, r * 256:r * 256 + 128, :]), f"ld_hb{slot}", reads=[R_exout[e]], writes=[R_hb[slot]], partial=True)
                S.dma("sp", DMA(v[:, r * 16:(r + 1) * 16, :], exout_d[e, r * 256 + 128:r * 256 + 256, :].rearrange("p (b d) -> p b d", d=128)),
                      f"ld_hb{slot}", reads=[R_exout[e]], writes=[R_hb[slot]], partial=True)

        acc = [sb(f"acc{i}", [128, TW], F32) for i in range(2)]
        R_acc = [Res("acc0"), Res("acc1")]
        sqt = sb("sqt", [128, TW], BF16)
        R_sqt = Res("sqt")
        ones_f = mats[:, 256:384]
        qt_ctr = [0]
        pending = []

        def run_pending(item_no, force=False):
            keep = []
            for (due, fn) in pending:
                if force or item_no >= due:
                    fn(item_no)
                else:
                    keep.append((due, fn))
            pending[:] = keep

        def diff_head(l, h, slot):
            q, q2, k, v = hb(slot)
            scale = 1.0 / math.sqrt(64.0)
            items = [(tt, i) for tt in range(NT) for i in range(len(attn_keytiles(tt)))]

            def scores(j):
                tt, i = items[j]
                r, kb, mid = attn_keytiles(tt)[i]
                sb_ = (j % 2) * 2
                for half in range(2):
                    pr = slice(half * 64, half * 64 + 64)
                    S.op("pe", MM(ps[sb_ + half][:], k[pr, r, kb * 128:(kb + 1) * 128], q[pr, tt * TW:(tt + 1) * TW], True, True),
                         reads=[R_hb[slot]], writes=[R_ps[sb_ + half]])
            scores(0)
            for j, (tt, i) in enumerate(items):
                tsl = slice(tt * TW, (tt + 1) * TW)
                tiles = attn_keytiles(tt)
                n = len(tiles)
                r, kb, mid = tiles[i]
                sb_ = (j % 2) * 2
                if j + 1 < len(items):
                    scores(j + 1)
                for half in range(2):
                    p = nxt("pb", NPB)
                    src = ps[sb_ + half][:]
                    rd = [R_ps[sb_ + half]]
                    if mid is not None:
                        w = nxt("wk", 4)
                        S.op("dve", TT(wk[w][:, 0:TW], ps[sb_ + half][:], masks[:, mid * TW:(mid + 1) * TW], ALU.add),
                             reads=[R_ps[sb_ + half], R_masks], writes=[R_wk[w]])
                        src = wk[w][:, 0:TW]
                        rd = [R_wk[w]]
                    S.op("act", ACT(pb[p][:], src, AF.Exp, scale=scale), reads=rd, writes=[R_pb[p]])
                    S.op("pe", MM(ps[4 + half][:], v[:, r * 16 + kb, :], pb[p][:], i == 0, i == n - 1), reads=[R_pb[p], R_hb[slot]], writes=[R_ps[4 + half]])
                    S.op("pe", MM(ps[6 + half][:], ones_bf, pb[p][:], i == 0, i == n - 1), reads=[R_pb[p], R_const], writes=[R_ps[6 + half]])
                run_pending(j)
                if i == n - 1:
                    S.op("dve", CP(acc[0][:], ps[6][:]), reads=[R_ps[6]], writes=[R_acc[0]])
                    S.op("dve", CP(acc[1][:], ps[7][:]), reads=[R_ps[7]], writes=[R_acc[1]])
                    S.op("dve", CP(wk[4][:, 0:TW], ps[4][:]), reads=[R_ps[4]], writes=[R_wk[4]])
                    S.op("dve", CP(wk[5][:, 0:TW], ps[5][:]), reads=[R_ps[5]], writes=[R_wk[5]])

                    def st_a(item_no):
                        for a_ in range(2):
                            S.op("act", ACT(acc[a_][:], acc[a_][:], AF.Ln), reads=[R_acc[a_]], writes=[R_acc[a_]])
                            S.op("act", ACT(acc[a_][:], acc[a_][:], AF.Exp, scale=-1.0), reads=[R_acc[a_]], writes=[R_acc[a_]])

                    def st_b(item_no, l=l):
                        S.op("dve", TT(wk[4][:, 0:TW], wk[4][:, 0:TW], acc[0][:], ALU.mult), reads=[R_wk[4], R_acc[0]], writes=[R_wk[4]])
                        S.op("dve", TT(wk[5][:, 0:TW], wk[5][:, 0:TW], acc[1][:], ALU.mult), reads=[R_wk[5], R_acc[1]], writes=[R_wk[5]])
                        S.op("dve", STT(wk[4][:, 0:TW], wk[5][:, 0:TW], neglam(l), wk[4][:, 0:TW], ALU.mult, ALU.add), reads=[R_wk[4], R_wk[5], R_const], writes=[R_wk[4]])

                    def st_c(item_no):
                        S.op("act", ACT(sqt[:], wk[4][:, 0:TW], AF.Square), reads=[R_wk[4]], writes=[R_sqt])

                    def st_d(item_no, h=h, tsl=tsl, tt=tt, l=l):
                        bk = (item_no % 2) * 2
                        S.op("pe", MM(ps[bk][:], ones_bf, sqt[:], True, True), reads=[R_sqt, R_const], writes=[R_ps[bk]])
                        rstd_from_bank(bk, 128, 5)
                        S.op("dve", STT(A3[:, h, tsl], wk[4][:, 0:TW], gsubs[:, l:l + 1], wk[5][:, 0:TW], ALU.mult, ALU.mult),
                             reads=[R_wk[4], R_wk[5], R_const], writes=[R_A[h][tt]])
                    if j + 7 < len(items):
                        pending.append((j + 2, st_a))
                        pending.append((j + 3, st_b))
                        pending.append((j + 4, st_c))
                        pending.append((j + 6, st_d))
                    else:
                        st_a(j); st_b(j); st_c(j); st_d(j)
            run_pending(len(items), force=True)

        def mla_head(l, h, slot):
            q, q2, k, v = hb(slot)
            scale = 1.0 / math.sqrt(192.0)
            pr = slice((h % 2) * 64, (h % 2) * 64 + 64)
            items = [(tt, i) for tt in range(NT) for i in range(len(attn_keytiles(tt)))]
            NI = len(items)

            def scores(j):
                tt, i = items[j]
                r, kb, mid = attn_keytiles(tt)[i]
                sb_ = j % 4
                tsl = slice(tt * TW, (tt + 1) * TW)
                S.op("pe", MM(ps[sb_][:], k[:, r, kb * 128:(kb + 1) * 128], q[:, tsl], True, False), reads=[R_hb[slot]], writes=[R_ps[sb_]])
                S.op("pe", MM(ps[sb_][:], kpe_sb[pr, r, kb * 128:(kb + 1) * 128], q2[pr, tsl], False, True), reads=[R_hb[slot], R_kpe], writes=[R_ps[sb_]])
            scores(0)
            scores(1)
            for j, (tt, i) in enumerate(items):
                tsl = slice(tt * TW, (tt + 1) * TW)
                tiles = attn_keytiles(tt)
                n = len(tiles)
                r, kb, mid = tiles[i]
                sb_ = j % 4
                ob = 4 + (qt_ctr[0] % 2)
                lb = 6 + (qt_ctr[0] % 2)
                if j + 2 < NI:
                    scores(j + 2)
                p = nxt("pb", NPB)
                src = ps[sb_][:]
                rd = [R_ps[sb_]]
                if mid is not None:
                    w = nxt("wk", 4)
                    S.op("dve", TT(wk[w][:, 0:TW], ps[sb_][:], masks[:, mid * TW:(mid + 1) * TW], ALU.add), reads=[R_ps[sb_], R_masks], writes=[R_wk[w]])
                    src = wk[w][:, 0:TW]
                    rd = [R_wk[w]]
                S.op("act", ACT(pb[p][:], src, AF.Exp, scale=scale), reads=rd, writes=[R_pb[p]])
                S.op("pe", MM(ps[ob][:], v[:, r * 16 + kb, :], pb[p][:], i == 0, i == n - 1), reads=[R_pb[p], R_hb[slot]], writes=[R_ps[ob]])
                S.op("pe", MM(ps[lb][:], ones_bf, pb[p][:], i == 0, i == n - 1), reads=[R_pb[p], R_const], writes=[R_ps[lb]])
                run_pending(j)
                if i == n - 1:
                    w4 = 4 + (qt_ctr[0] % 2)

                    def m_a(item_no, w4=w4, lb=lb):
                        S.op("act", ACT(wk[w4][:, 0:TW], ps[lb][:], AF.Ln), reads=[R_ps[lb]], writes=[R_wk[w4]])
                        S.op("act", ACT(wk[w4][:, 0:TW], wk[w4][:, 0:TW], AF.Exp, scale=-1.0), reads=[R_wk[w4]], writes=[R_wk[w4]])

                    def m_b(item_no, w4=w4, ob=ob, h=h, tsl=tsl, tt=tt):
                        S.op("dve", TT(A3[:, 8 + h, tsl], ps[ob][:], wk[w4][:, 0:TW], ALU.mult), reads=[R_ps[ob], R_wk[w4]], writes=[R_A[8 + h][tt]])
                    if j + 3 < NI:
                        pending.append((j + 1, m_a))
                        pending.append((j + 2, m_b))
                    else:
                        m_a(j); m_b(j)
                    qt_ctr[0] += 1
            run_pending(NI, force=True)

        R_memx = [Res("memx0"), Res("memx1")]
        R_kst, R_vst = Res("kst"), Res("vst")
        memx = [B32[:, 8192:8192 + 2048], B32[:, 8192 + 2048:8192 + 4096]]
        mnT = B32[:, 0:4096].rearrange("p (c t) -> p c t", t=256)
        R_mnT = Res("mnT")
        R_Bmisc.extend(R_memx + [R_kst, R_vst, R_mnT])
        for blk in range(2):
            S.dma("sp", DMA(memx[blk], mem_in[blk * 128:(blk + 1) * 128, :]), "ld_yt", writes=[R_memx[blk]])
        for c in range(DC):
            b = c % 4
            for blk in range(2):
                S.op("pe", TR(ps[b][:, blk * 128:(blk + 1) * 128], memx[blk][:, c * 128:(c + 1) * 128], ident), reads=[R_memx[blk], R_const], writes=[R_ps[b]])
            S.op("act", ACT(mnT[:, c, :], ps[b][:, 0:256], AF.Copy), reads=[R_ps[b]], writes=[R_mnT])
            p = nxt("pb", NPB)
            S.op("act", ACT(pb[p][:, 0:256], mnT[:, c, :], AF.Square), reads=[R_mnT], writes=[R_pb[p]])
            S.op("pe", MM(ps[5][:, 0:256], ones_bf, pb[p][:, 0:256], c == 0, c == DC - 1), reads=[R_pb[p], R_const], writes=[R_ps[5]])
        S.op("act", ACT(wk[7][:, 0:256], ps[5][:, 0:256], AF.Sqrt, bias=cst[:, 4:5], scale=1.0 / D), reads=[R_ps[5], R_const], writes=[R_wk[7]])
        S.op("dve", RCP(wk[7][:, 0:256], wk[7][:, 0:256]), reads=[R_wk[7]], writes=[R_wk[7]])
        mT = Bb[:, 16384:16384 + DC * 256].rearrange("p (c t) -> p c t", t=256)
        R_mT = Res("mT")
        R_Bmisc.append(R_mT)
        for l in range(L):
            for c in range(DC):
                S.op("dve", STT(mT[:, c, :], mnT[:, c, :], V(l, V_GMEM, c), wk[7][:, 0:256], ALU.mult, ALU.mult), reads=[R_mnT, R_wk[7], R_const], writes=[R_mT])
            kst = Bb[:, 24576:24576 + DC * 256].rearrange("p (c t) -> p c t", t=256)
            for cb in range(8):
                i, wv = load_w(w_kvx[l], DC, cb * 256, 256)
                for oc in range(2):
                    b = (cb * 2 + oc) % 4
                    for kc in range(DC):
                        S.op("pe", MM(ps[b][:, 0:256], wv[:, kc, oc * 128:(oc + 1) * 128], mT[:, kc, :], kc == 0, kc == DC - 1),
                             reads=[R_WB[i], R_mT] if kc in (0, DC - 1) else (), writes=[R_ps[b]])
                    S.op("act", ACT(kst[:, cb * 2 + oc, :], ps[b][:, 0:256], AF.Copy), reads=[R_ps[b]], writes=[R_kst])
            S.dma("sp", DMA(kmT_d[l], kst), "st_kst", reads=[R_kst], writes=[R_kmT[l]])
            vst = Bb[:, 28672:28672 + 2 * D].rearrange("p (b d) -> p b d", d=D)
            for nb4 in range(8):
                i, wv = load_w(w_kvx[l], DC, D + nb4 * 256, 256)
                for blk in range(2):
                    b = 4 + blk
                    for kc in range(DC):
                        S.op("pe", MM(ps[b][:, 0:256], mT[:, kc, blk * 128:(blk + 1) * 128], wv[:, kc, :], kc == 0, kc == DC - 1),
                             reads=[R_WB[i], R_mT] if kc in (0, DC - 1) else (), writes=[R_ps[b]])
                    S.op("act", ACT(vst[:, blk, nb4 * 256:(nb4 + 1) * 256], ps[b][:, 0:256], AF.Copy), reads=[R_ps[b]], writes=[R_vst])
            S.dma("sp", DMA(vm_d[l], vst), "st_vst", reads=[R_vst], writes=[R_vm[l]])

        nr_pass(0, "init", 0, 0, V_GPM)
        for l in range(L_RUN):
            S.dma("sp", DMA(B32[:, 0:2048], rope_d[0]), "ld_ropes", reads=[R_rope], writes=[R_ropes] + RB_all())
            S.dma("sp", DMA(B32[:, 2048:4096], rope_d[1]), "ld_ropes", reads=[R_rope], writes=[R_ropes], partial=True)
            W = w_in[l]
            gb = [0, 1, 2, 3]
            gemm_fm(W, DC, 0, 8, hT_ap, hT_res, consume_rope(lambda oc: exin_d[oc, 0:128, :], lambda oc: R_exin[oc]), gb, head_tt=True)
            for g in range(4):
                gemm_tm(W, DC, 1024 + g * 256, 256, lambda kc, blk: A3[:, kc, blk * 128:(blk + 1) * 128], lambda blk: RA_tt(blk // 4),
                        consume_tokmajor_V(g * 2, 2), [4, 5])
            for h in range(8):
                S.dma("pool", CC(exin_d[h], exout_d[h]), "cc", reads=[R_exin[h]], writes=[R_exout[h]], inc=1)
            gemm_fm(W, DC, 16, 2, hT_ap, hT_res, consume_latent(l, 2, ckvn, R_ckvn, V_GCKV, 5), gb, tt_outer=True)
            gemm_fm(W, DC, 18, 1, hT_ap, hT_res, consume_rope(lambda oc: kpe_in_d[:, :], lambda oc: R_kpein), gb)
            S.dma("pool", CC(kpe_in_d[:, :], kpe_out_d[:, :]), "cc", reads=[R_kpein], writes=[R_kpeout], inc=1)
            ckv_ap = lambda kc, tt: ckvn[:, kc, tt * TW:(tt + 1) * TW]
            ckv_res = lambda tt: [R_ckvn[tt]]
            gemm_fm(w_ukv[l], 2, 0, 8, ckv_ap, ckv_res, consume_store_bf16(lambda oc: exin_d[8 + oc, 0:128, :], lambda oc: R_exin[8 + oc]), gb)
            for g in range(2):
                gemm_tm(w_ukv[l], 2, 1024 + g * 512, 512, lambda kc, blk: ckvn[:, kc, blk * 128:(blk + 1) * 128], lambda blk: [R_ckvn[blk // 4]],
                        consume_tokmajor_V(8 + g * 4, 4), [4, 5])
            for h in range(8):
                S.dma("pool", CC(exin_d[8 + h], exout_d[8 + h]), "cc", reads=[R_exin[8 + h]], writes=[R_exout[8 + h]], inc=1)
            gemm_fm(W, DC, 19, 8, hT_ap, hT_res, consume_rope(lambda oc: qd_d[oc], lambda oc: R_qd[oc]), gb)
            gemm_fm(W, DC, 27, 4, hT_ap, hT_res, consume_latent(l, 4, cqn, R_cqn, V_GCQ, 5), gb, tt_outer=True)
            cq_ap = lambda kc, tt: cqn[:, kc, tt * TW:(tt + 1) * TW]
            cq_res = lambda tt: [R_cqn[tt]]
            gemm_fm(w_uq[l], 4, 0, 8, cq_ap, cq_res, consume_store_bf16(lambda oc: qmn_d[oc], lambda oc: R_qmn[oc]), gb)
            gemm_fm(w_uq[l], 4, 8, 4, cq_ap, cq_res, consume_rope(lambda oc: qmp_d[oc], lambda oc: R_qmp[oc]), gb)
            gemm_fm(W, DC, 31, 32, hT_ap, hT_res,
                    consume_store_bf16(lambda oc: gates_d[oc], lambda oc: R_gates[oc], func=AF.Sigmoid, bias_fn=lambda oc: V(l, V_BG, oc)), gb)

            S.dma("sp", DMA(kpe_sb[:, 0, :], kpe_out_d[0:128, :]), "ld_kpe", reads=[R_kpeout], writes=[R_kpe] + RB_all())
            S.dma("sp", DMA(kpe_sb[:, 1, :], kpe_out_d[128:256, :]), "ld_kpe", reads=[R_kpeout], writes=[R_kpe], partial=True)
            S.dma("pool", DMA(masks, mask_in[:, :]), "ld_masks", writes=[R_masks])
            heads = [("diff", h) for h in range(8)] + [("mla", h) for h in range(8)]
            load_head(0, heads[0][1], heads[0][0])
            for hi, (kind, h) in enumerate(heads):
                slot = hi % 2
                if hi + 1 < len(heads):
                    load_head(1 - slot, heads[hi + 1][1], heads[hi + 1][0])
                if kind == "diff":
                    diff_head(l, h, slot)
                else:
                    mla_head(l, h, slot)

            if DBG:
                S.dma("sp", DMA(dbgA_d[0], A[:]), "st_dbgA", reads=RA_all(), writes=[R_dbg], partial=True)
            first_b = True
            p4_tiles = [(c, tt) for c in range(DC) for tt in range(NT)]
            gate_tiles = {}

            def load_gates(idx):
                if idx >= len(p4_tiles) or idx in gate_tiles:
                    return
                c, tt = p4_tiles[idx]
                tsl = slice(tt * TW, (tt + 1) * TW)
                ga, gbt = nxt("pb", NPB), nxt("pb", NPB)
                S.dma("sp", DMA(pb[ga][:], gates_d[c, :, tsl]), f"ld_pb{ga}", reads=[R_gates[c]], writes=[R_pb[ga]])
                S.dma("sp", DMA(pb[gbt][:], gates_d[16 + c, :, tsl]), f"ld_pb{gbt}", reads=[R_gates[16 + c]], writes=[R_pb[gbt]])
                gate_tiles[idx] = (ga, gbt)
            load_gates(0)
            load_gates(1)
            c0 = 0
            tix = 0
            while c0 < DC:
                nblk = min(3, DC - c0)
                i = nxt("wb", 2)
                load_w(w_brd[l], 8, c0 * 128, nblk * 128, slot_kc0=0, buf=i, tot_kc=16)
                _, wv = load_w(w_brm[l], 8, c0 * 128, nblk * 128, slot_kc0=8, buf=i, tot_kc=16)
                for oc in range(nblk):
                    c = c0 + oc
                    for tt in range(NT):
                        tsl = slice(tt * TW, (tt + 1) * TW)
                        ba, bb = [(0, 1), (2, 3), (4, 5), (6, 7)][tix % 4]
                        load_gates(tix + 2)
                        for kc in range(8):
                            S.op("pe", MM(ps[ba][:], wv[:, kc, oc * 128:(oc + 1) * 128], A3[:, kc, tsl], kc == 0, kc == 7),
                                 reads=([R_WB[i]] + [R_A[k][tt] for k in range(8)]) if kc in (0, 7) else (), writes=[R_ps[ba]])
                        for kc in range(8):
                            S.op("pe", MM(ps[bb][:], wv[:, 8 + kc, oc * 128:(oc + 1) * 128], A3[:, 8 + kc, tsl], kc == 0, kc == 7),
                                 reads=([R_WB[i]] + [R_A[8 + k][tt] for k in range(8)]) if kc in (0, 7) else (), writes=[R_ps[bb]])
                        ga, gbt = gate_tiles.pop(tix)
                        w = nxt("wk", 4)
                        S.op("dve", TT(wk[w][:, 0:TW], ps[ba][:], pb[ga][:], ALU.mult), reads=[R_ps[ba], R_pb[ga]], writes=[R_wk[w]])
                        w2 = 4 + (w % 2)
                        S.op("dve", TT(wk[w2][:, 0:TW], ps[bb][:], pb[gbt][:], ALU.mult), reads=[R_ps[bb], R_pb[gbt]], writes=[R_wk[w2]])
                        S.op("dve", TT(Bb3[:, c, tsl], wk[w][:, 0:TW], wk[w2][:, 0:TW], ALU.add), reads=[R_wk[w], R_wk[w2]],
                             writes=[R_B[c][tt]] + (RB_all() if first_b else []))
                        first_b = False
                        tix += 1
                c0 += nblk

            def consume_y(res_c_off=0):
                def consume(oc, tt, b, deferred):
                    w = nxt("wk", 4)
                    S.op("act", ACT(wk[w][:, 0:TW], ps[b][:], AF.Copy), reads=[R_ps[b]], writes=[R_wk[w]])
                    S.dma("sp", DMA(yT_d[:, oc, tt * TW:(tt + 1) * TW], wk[w][:, 0:TW]), f"st_wk{w}", reads=[R_wk[w]], writes=[R_yT[oc][tt]])
                return consume
            B_ap = lambda kc, tt: Bb3[:, kc, tt * TW:(tt + 1) * TW]
            B_res = lambda tt: [R_B[c][tt] for c in range(DC)]
            gemm_fm(w_mo[l], DC, 0, DC, B_ap, B_res, consume_y(), [0, 1, 2, 3, 4, 5, 6, 7])
            nr_pass(l, "mid", V_GPOM, l, V_GPX)

            def consume_qx(oc, tt, b, deferred):
                S.op("act", ACT(Bb3[:, oc, tt * TW:(tt + 1) * TW], ps[b][:], AF.Copy), reads=[R_ps[b]],
                     writes=[R_B[oc][tt]] + (RB_all() if (oc == 0 and tt == 0) else []))
            gemm_fm(w_qx[l], DC, 0, DC, hT_ap, hT_res, consume_qx, [0, 1, 2, 3, 4, 5, 6, 7], head_tt=True)
            kmv = WB[0][:, 0:DC * 256].rearrange("p (c t) -> p c t", t=256)
            vmv = WB[1][:, 0:2 * D].rearrange("p (b d) -> p b d", d=D)
            S.dma("sp", DMA(kmv, kmT_d[l]), "ld_wb0", reads=[R_kmT[l]], writes=[R_WB[0]])
            S.dma("sp", DMA(vmv, vm_d[l]), "ld_wb1", reads=[R_vm[l]], writes=[R_WB[1]])
            xscale = 1.0 / math.sqrt(512.0)
            steps = [(tt, hx) for tt in range(NT) for hx in range(4)]
            ptiles = {}

            def x_scores(g):
                tt, hx = steps[g]
                tsl = slice(tt * TW, (tt + 1) * TW)
                sbase = 0 if g % 2 == 0 else 6
                pts = []
                for kt in range(2):
                    sbk = sbase + kt
                    for c4 in range(4):
                        c = hx * 4 + c4
                        S.op("pe", MM(ps[sbk][:], kmv[:, c, kt * 128:(kt + 1) * 128], Bb3[:, c, tsl], c4 == 0, c4 == 3),
                             reads=[R_WB[0], R_B[c][tt]], writes=[R_ps[sbk]])
                    p = nxt("pb", NPB)
                    S.op("act", ACT(pb[p][:], ps[sbk][:], AF.Exp, scale=xscale), reads=[R_ps[sbk]], writes=[R_pb[p]])
                    pts.append(p)
                ptiles[g] = pts
            x_scores(0)
            for g, (tt, hx) in enumerate(steps):
                tsl = slice(tt * TW, (tt + 1) * TW)
                if g + 1 < len(steps):
                    x_scores(g + 1)
                pts = ptiles.pop(g)
                lbk = 0 if g % 2 == 0 else 6
                for kt in range(2):
                    S.op("pe", MM(ps[lbk][:], ones_bf, pb[pts[kt]][:], kt == 0, kt == 1), reads=[R_pb[pts[kt]], R_const], writes=[R_ps[lbk]])
                for c4 in range(4):
                    c = hx * 4 + c4
                    for kt in range(2):
                        S.op("pe", MM(ps[2 + c4][:], vmv[:, kt, c * 128:(c + 1) * 128], pb[pts[kt]][:], kt == 0, kt == 1),
                             reads=[R_pb[pts[kt]], R_WB[1]], writes=[R_ps[2 + c4]])
                w = 4 + (g % 2)
                S.op("act", ACT(wk[w][:, 0:TW], ps[lbk][:], AF.Ln), reads=[R_ps[lbk]], writes=[R_wk[w]])
                S.op("act", ACT(wk[w][:, 0:TW], wk[w][:, 0:TW], AF.Exp, scale=-1.0), reads=[R_wk[w]], writes=[R_wk[w]])
                for c4 in range(4):
                    c = hx * 4 + c4
                    S.op("dve", TT(A3[:, c, tsl], ps[2 + c4][:], wk[w][:, 0:TW], ALU.mult), reads=[R_ps[2 + c4], R_wk[w]], writes=[R_A[c][tt]])

            if DBG:
                S.dma("sp", DMA(dbgA_d[1], A[:]), "st_dbgA", reads=RA_all(), writes=[R_dbg], partial=True)
            gemm_fm(w_ox[l], DC, 0, DC, hT_ap, hT_res, consume_y(), [0, 1, 2, 3, 4, 5, 6, 7])
            nr_pass(l, "mid", V_GPOX, l, V_GPF)

            hc4 = hc[:].rearrange("p (c b t) -> p c b t", b=16, t=2)
            S.op("dve", CP(hc4, A[:].rearrange("p (c b t) -> p c b t", b=16, t=128)[:, :, :, 126:128]), reads=RA_all(), writes=[R_hc])
            S.dma("sp", DMA(halo_in_d[:, :], hc[:]), "st_hc", reads=[R_hc], writes=[R_haloin])
            S.dma("pool", CC(halo_in_d[:, :], halo_out_d[:, :]), "cc", reads=[R_haloin], writes=[R_haloout], inc=1)
            S.dma("sp", DMA(hg[:, 0:512], halo_out_d[0:128, :]), "ld_hg", reads=[R_haloout], writes=[R_hg])
            S.dma("sp", DMA(hg[:, 512:1024], halo_out_d[128:256, :]), "ld_hg", reads=[R_haloout], writes=[R_hg], partial=True)
            g0 = hg[:, 0:512].rearrange("p (c b t) -> p c b t", b=16, t=2)
            g1 = hg[:, 512:1024].rearrange("p (c b t) -> p c b t", b=16, t=2)
            hh4 = hh[:].rearrange("p (c b t) -> p c b t", b=16, t=2)
            hh3 = hh[:].rearrange("p (c n) -> p c n", n=32)
            S.op("dve", TS(hh4, g0, cst[:, 6:7], None, ALU.mult), reads=[R_hg, R_const], writes=[R_hh])
            S.op("dve", STT(hh4[:, :, 1:16, :], g1[:, :, 0:15, :], cst[:, 5:6], hh4[:, :, 1:16, :], ALU.mult, ALU.add), reads=[R_hg, R_hh, R_const], writes=[R_hh])

            first_b = True
            for si, (p0, p1) in enumerate(FFN_SPLITS):
                npair = p1 - p0
                uev = [wk[i][:, 0:520].rearrange("p (b t) -> p b t", t=130) for i in range(NWK)]
                for blk0 in range(0, npair, 1):
                    nb2 = 1
                    i, wv = load_w(w_up[l], DC, (p0 + blk0) * 256, nb2 * 256)
                    for pj in range(nb2):
                        j = p0 + blk0 + pj
                        hb_ = 6
                        for half in range(2):
                            for kc in range(DC):
                                S.op("pe", MM(ps[hb_][:, half * 32:half * 32 + 32], wv[:, kc, (pj * 2 + half) * 128:(pj * 2 + half + 1) * 128], hh3[:, kc, :], kc == 0, kc == DC - 1),
                                     reads=[R_WB[i], R_hh] if kc in (0, DC - 1) else (), writes=[R_ps[hb_]])
                        S.op("act", ACT(wk[6][:, 0:64], ps[hb_][:, 0:64], AF.Copy), reads=[R_ps[hb_]], writes=[R_wk[6]])
                        for tt in range(NT):
                            tsl = slice(tt * TW, (tt + 1) * TW)
                            cres = []
                            for half in range(2):
                                ch = j + 44 * half
                                b = (tt * 2 + half) % 4
                                for kc in range(DC):
                                    S.op("pe", MM(ps[b][:], wv[:, kc, (pj * 2 + half) * 128:(pj * 2 + half + 1) * 128], A3[:, kc, tsl], kc == 0, kc == DC - 1),
                                         reads=[R_WB[i]] + RA_tt(tt) if kc in (0, DC - 1) else (), writes=[R_ps[b]])
                                u = half * 2 + (tt % 2)
                                S.op("act", ACT(uev[u][:, :, 2:130], ps[b][:].rearrange("p (b t) -> p b t", t=128), AF.Copy), reads=[R_ps[b]], writes=[R_wk[u]])
                                S.op("act", ACT(uev[u][:, :, 0:2], wk[6][:, half * 32 + tt * 8: half * 32 + tt * 8 + 8].rearrange("p (b t) -> p b t", t=2), AF.Copy),
                                     reads=[R_wk[6], R_wk[u]], writes=[R_wk[u]])
                                cw = 4 + half
                                cv = wk[cw][:, 0:TW].rearrange("p (b t) -> p b t", t=128)
                                S.op("dve", TS(cv, uev[u][:, :, 0:128], V(l, V_CW0, ch), V(l, V_CB, ch), ALU.mult, ALU.add), reads=[R_wk[u], R_const], writes=[R_wk[cw]])
                                S.op("dve", STT(cv, uev[u][:, :, 1:129], V(l, V_CW1, ch), cv, ALU.mult, ALU.add), reads=[R_wk[u], R_wk[cw], R_const], writes=[R_wk[cw]])
                                S.op("dve", STT(cv, uev[u][:, :, 2:130], V(l, V_CW2, ch), cv, ALU.mult, ALU.add), reads=[R_wk[u], R_wk[cw], R_const], writes=[R_wk[cw]])
                            S.op("act", ACT(wk[4][:, 0:TW], wk[4][:, 0:TW], AF.Silu), reads=[R_wk[4]], writes=[R_wk[4]])
                            jl = j - p0
                            S.op("dve", TT(Bb3[:, jl, tsl], wk[4][:, 0:TW], wk[5][:, 0:TW], ALU.mult), reads=[R_wk[4], R_wk[5]],
                                 writes=[R_B[jl][tt]] + (RB_all() if first_b else []))
                            first_b = False
                def consume_down(oc, tt, b, deferred, si=si):
                    w = nxt("wk", 4)
                    if si == 0:
                        S.op("act", ACT(wk[w][:, 0:TW], ps[b][:], AF.Copy), reads=[R_ps[b]], writes=[R_wk[w]])
                    else:
                        S.dma("sp", DMA(wk[w][:, 0:TW], yT_d[:, oc, tt * TW:(tt + 1) * TW]), f"ld_wk{w}", reads=[R_yT[oc][tt]], writes=[R_wk[w]])
                        S.op("dve", TT(wk[w][:, 0:TW], ps[b][:], wk[w][:, 0:TW], ALU.add), reads=[R_ps[b], R_wk[w]], writes=[R_wk[w]])
                    S.dma("sp", DMA(yT_d[:, oc, tt * TW:(tt + 1) * TW], wk[w][:, 0:TW]), f"st_wk{w}", reads=[R_wk[w]], writes=[R_yT[oc][tt]])
                act_res = lambda tt, npair=npair: [R_B[c][tt] for c in range(npair)]
                gemm_fm(w_dn[l], npair, 0, DC, B_ap, act_res, consume_down, [0, 1, 2, 3, 4, 5, 6, 7], row0=p0 * 128)
            if l < L - 1 or DBG:
                nr_pass(l, "mid", V_GPOF, l + 1, V_GPM)
            else:
                nr_pass(l, "final", V_GPOF, l, V_GPM)

        S._waits("sp", [R_out, R_dbg], [R_out, R_dbg])

        sems_e = {e: es.enter_context(nc.semaphore(f"s_{e}")) for e in ENGS}
        sems_d = {s: es.enter_context(nc.semaphore(f"d_{s}")) for s in S.dma_sems}
        block = es.enter_context(nc.Block())
        S.emit(block, sems_e, sems_d)
    return nc


def _vec_cols(v):
    return np.ascontiguousarray(v.reshape(-1, 128).T)


def kernel(x, mem, positions, g_pre_mix, w_in, b_gate, lam_q1, lam_k1, lam_q2, lam_k2,
           g_diff_sub, g_cq, w_uq, g_ckv, w_ukv, w_br_diff, w_br_mla, w_mix_out,
           g_post_mix, g_pre_x, g_mem, w_q_x, w_kv_x, w_o_x, g_post_x, g_pre_ffn,
           w_up, conv_w, conv_b, w_down, g_post_ffn):
    f32 = np.float32
    x = np.asarray(x, f32)
    mem = np.asarray(mem, f32)
    positions = np.asarray(positions, np.int32)
    A_ = lambda a: np.asarray(a, f32)
    w_in, w_uq, w_ukv, w_up = A_(w_in), A_(w_uq), A_(w_ukv), A_(w_up)
    idx_in = np.concatenate([np.arange(1024, 2048), np.arange(2048, 3072), np.arange(3584, 3840), np.arange(3840, 3904),
                             np.arange(3840, 3904), np.arange(0, 1024), np.arange(3072, 3584), np.arange(3904, 8000)])
    w_in_p = np.ascontiguousarray(w_in[:, :, idx_in])
    idx_uq = np.concatenate([np.concatenate([np.arange(h * 192, h * 192 + 128) for h in range(8)]),
                             np.concatenate([np.arange(h * 192 + 128, h * 192 + 192) for h in range(8)])])
    w_uq_p = np.ascontiguousarray(w_uq[:, :, idx_uq])
    idx_ukv = np.concatenate([np.concatenate([np.arange(h * 256, h * 256 + 128) for h in range(8)]),
                              np.concatenate([np.arange(h * 256 + 128, h * 256 + 256) for h in range(8)])])
    w_ukv_p = np.ascontiguousarray(w_ukv[:, :, idx_ukv])
    idx_up = np.concatenate([np.concatenate([np.arange(j * 128, j * 128 + 128), np.arange(5632 + j * 128, 5632 + j * 128 + 128)]) for j in range(44)])
    w_up_p = np.ascontiguousarray(w_up[:, :, idx_up])

    vecs = np.zeros((128, L * NVL), f32)
    cw = A_(conv_w)
    for l in range(L):
        b = l * NVL
        for off, v in ((V_GPM, g_pre_mix), (V_GPOM, g_post_mix), (V_GPX, g_pre_x), (V_GPOX, g_post_x), (V_GPF, g_pre_ffn),
                       (V_GPOF, g_post_ffn), (V_GMEM, g_mem), (V_BG, b_gate), (V_GCQ, g_cq), (V_GCKV, g_ckv), (V_GSUB, g_diff_sub),
                       (V_CB, conv_b)):
            c = _vec_cols(A_(v)[l])
            vecs[:, b + off:b + off + c.shape[1]] = c
        for k, off in enumerate((V_CW0, V_CW1, V_CW2)):
            c = _vec_cols(cw[l, k])
            vecs[:, b + off:b + off + 88] = c
    lamv = np.concatenate([np.concatenate([A_(lam_q1)[l], A_(lam_k1)[l], A_(lam_q2)[l], A_(lam_k2)[l]]) for l in range(L)])[None, :].astype(f32)

    inv = (10000.0 ** (-np.arange(0, 64, 2, dtype=np.float32) / 64.0)).astype(f32)
    mats = np.zeros((128, 384), f32)
    mats[:, 0:128] = np.eye(128, dtype=f32)
    for i in range(128):
        j = (i % 64 + 32) % 64 + (i // 64) * 64
        mats[j, 128 + i] = 1.0
    mats[:, 256:384] = 1.0

    in_maps = []
    for c in range(8):
        b, r = c // 2, c % 2
        xs = np.ascontiguousarray(x[b].reshape(32, 128, D)[r::2].reshape(T, D))
        ps_ = np.ascontiguousarray(positions[b].reshape(32, 128)[r::2].reshape(1, T))
        cst = np.zeros((128, 8), f32)
        pidx = np.arange(128)
        cst[:, 0] = inv[pidx % 32]
        cst[:, 1] = np.where((pidx % 64) < 32, -1.0, 1.0)
        cst[:, 2] = -1.0
        cst[:, 3] = -math.pi
        cst[:, 4] = EPS
        cst[:, 5] = 1.0 if r == 0 else 0.0
        cst[:, 6] = 1.0 if r == 1 else 0.0
        masks = np.zeros((128, 8, TW), f32)
        kp = np.arange(128)
        qq = np.arange(TW)
        for rk in range(2):
            for jj in range(4):
                kchunk = 2 * (2 * jj + rk) + (kp >= 64).astype(np.int64)
                qchunk = 2 * (2 * (qq // 128) + r) + ((qq % 128) >= 64).astype(np.int64)
                vis = kchunk[:, None] <= qchunk[None, :]
                masks[:, rk * 4 + jj, :] = np.where(vis, 0.0, NEG)
        in_maps.append({
            "x": xs, "mem": np.ascontiguousarray(mem[b]), "pos": ps_, "vecs": vecs, "lamv": lamv, "cst": cst,
            "masks": np.ascontiguousarray(masks.reshape(128, 8 * TW)), "mats": mats,
            "w_in": w_in_p, "w_uq": w_uq_p, "w_ukv": w_ukv_p, "w_br_diff": A_(w_br_diff), "w_br_mla": A_(w_br_mla),
            "w_mix_out": A_(w_mix_out), "w_q_x": A_(w_q_x), "w_kv_x": A_(w_kv_x), "w_o_x": A_(w_o_x),
            "w_up": w_up_p, "w_down": A_(w_down),
        })
    nc = build_program()
    res = run_bass_kernel_spmd(nc, in_maps, core_ids=list(range(8)))
    if _DBG["on"]:
        _DBG["data"] = [np.asarray(res.results[c]["dbg"]) for c in range(8)]
        _DBG["dataA"] = [np.asarray(res.results[c]["dbgA"]).astype(np.float32) for c in range(2)]
    out = np.zeros((4, 4096, D), f32)
    for c in range(8):
        b, r = c // 2, c % 2
        out[b].reshape(32, 128, D)[r::2] = np.asarray(res.results[c]["out"], f32).reshape(16, 128, D)
    return out
```

```python
import math
import contextlib
import numpy as np
import concourse.bass as bass
import concourse.mybir as mybir
from concourse.bass_utils import run_bass_kernel_spmd

F32 = mybir.dt.float32
BF16 = mybir.dt.bfloat16
I32 = mybir.dt.int32
AF = mybir.ActivationFunctionType
ALU = mybir.AluOpType
AX = mybir.AxisListType

L = 4
D = 2048
T = 2048
NT = 4
TW = 512
NB = 16
DC = 16
EPS = 1e-6
NIN = 8064
NVL = 503
V_GPM, V_GPOM, V_GPX, V_GPOX, V_GPF, V_GPOF, V_GMEM, V_BG, V_GCQ, V_GCKV, V_GSUB, V_CW0, V_CW1, V_CW2, V_CB = \
    0, 16, 32, 48, 64, 80, 96, 112, 144, 148, 150, 151, 239, 327, 415
NEG = -30000.0
FFN_SPLITS = [(0, 12), (12, 24), (24, 34), (34, 44)]
PAIRS = [[0, 1], [2, 3], [4, 5], [6, 7]]


class Res:
    __slots__ = ("name", "w_ops", "w_dma", "r_ops", "r_dma")

    def __init__(self, name):
        self.name = name
        self.w_ops = {}
        self.w_dma = {}
        self.r_ops = {}
        self.r_dma = {}


ENGS = ("pe", "act", "dve", "pool", "sp")


class Sched:
    def __init__(self):
        self.items = {e: [] for e in ENGS}
        self.nops = {e: 0 for e in ENGS}
        self.signal = {e: set() for e in ENGS}
        self.known_op = {e: {f: 0 for f in ENGS} for e in ENGS}
        self.known_dma = {e: {} for e in ENGS}
        self.dma_cnt = {}
        self.dma_sems = []

    def _waits(self, eng, reads, writes, partial=False):
        need_op = {}
        need_dma = {}
        for r in reads:
            for f, i in r.w_ops.items():
                if i > need_op.get(f, 0):
                    need_op[f] = i
            for s, c in r.w_dma.items():
                if c > need_dma.get(s, 0):
                    need_dma[s] = c
        for r in writes:
            for d in ((r.r_ops,) if partial else (r.w_ops, r.r_ops)):
                for f, i in d.items():
                    if i > need_op.get(f, 0):
                        need_op[f] = i
            for d in ((r.r_dma,) if partial else (r.w_dma, r.r_dma)):
                for s, c in d.items():
                    if c > need_dma.get(s, 0):
                        need_dma[s] = c
        for f, i in need_op.items():
            if f == eng and eng == "pe":
                continue
            if i > self.known_op[eng][f]:
                self.known_op[eng][f] = i
                self.signal[f].add(i)
                self.items[eng].append(("wait_op", f, i))
        for s, c in need_dma.items():
            if s not in ("cc",):
                c = self.dma_cnt[s]
            if c > self.known_dma[eng].get(s, 0):
                self.known_dma[eng][s] = c
                self.items[eng].append(("wait_dma", s, c))

    def op(self, eng, fn, reads=(), writes=()):
        self._waits(eng, reads, writes)
        self.nops[eng] += 1
        idx = self.nops[eng]
        self.items[eng].append(("op", fn, idx))
        for r in reads:
            r.r_ops[eng] = idx
        for r in writes:
            r.w_ops = {eng: idx}
            r.w_dma = {}
            r.r_ops = {}
            r.r_dma = {}
        return idx

    def dma(self, eng, fn, sem, reads=(), writes=(), inc=16, partial=False):
        self._waits(eng, reads, writes, partial)
        c = self.dma_cnt.get(sem, 0) + inc
        self.dma_cnt[sem] = c
        if sem not in self.dma_sems:
            self.dma_sems.append(sem)
        self.items[eng].append(("dma", fn, sem, inc))
        for r in reads:
            r.r_dma[sem] = c
        for r in writes:
            if partial:
                r.w_dma[sem] = c
            else:
                r.w_ops = {}
                r.w_dma = {sem: c}
                r.r_ops = {}
                r.r_dma = {}

    def emit(self, block, sems_eng, sems_dma):
        sched = self
        counts = {}
        for e in ENGS:
            counts[e] = {idx: k + 1 for k, idx in enumerate(sorted(sched.signal[e]))}

        def run(eng, h):
            for it in sched.items[eng]:
                k = it[0]
                if k == "wait_op":
                    h.wait_ge(sems_eng[it[1]], counts[it[1]][it[2]])
                elif k == "wait_dma":
                    h.wait_ge(sems_dma[it[1]], it[2])
                elif k == "op":
                    ins = it[1](h)
                    if it[2] in counts[eng]:
                        ins.then_inc(sems_eng[eng], 1)
                else:
                    ins = it[1](h)
                    ins.then_inc(sems_dma[it[2]], it[3])

        @block.tensor
        def _(h):
            run("pe", h)

        @block.scalar
        def _(h):
            run("act", h)

        @block.vector
        def _(h):
            run("dve", h)

        @block.gpsimd
        def _(h):
            run("pool", h)

        @block.sync
        def _(h):
            run("sp", h)


def MM(out, lhsT, rhs, st, sp):
    return lambda h: h.matmul(out, lhsT, rhs, start=st, stop=sp)


def TR(out, in_, ident):
    return lambda h: h.transpose(out, in_, ident)


def ACT(out, in_, func, bias=None, scale=None):
    kw = {}
    if bias is not None:
        kw["bias"] = bias
    if scale is not None:
        kw["scale"] = scale
    return lambda h: h.activation(out=out, in_=in_, func=func, **kw)


def TS(out, in0, s1, s2, op0, op1=None):
    if op1 is None:
        return lambda h: h.tensor_scalar(out=out, in0=in0, scalar1=s1, scalar2=None, op0=op0)
    return lambda h: h.tensor_scalar(out=out, in0=in0, scalar1=s1, scalar2=s2, op0=op0, op1=op1)


def STT(out, in0, scalar, in1, op0, op1):
    return lambda h: h.scalar_tensor_tensor(out=out, in0=in0, scalar=scalar, in1=in1, op0=op0, op1=op1)


def TT(out, in0, in1, op):
    return lambda h: h.tensor_tensor(out=out, in0=in0, in1=in1, op=op)


def CP(out, in_):
    return lambda h: h.tensor_copy(out=out, in_=in_)


def RCP(out, in_):
    return lambda h: h.reciprocal(out=out, in_=in_)


def DMA(out, in_):
    return lambda h: h.dma_start(out=out, in_=in_)


def CC(in_ap, out_ap):
    return lambda h: h.collective_compute("AllGather", ALU.bypass, replica_groups=PAIRS, ins=[in_ap], outs=[out_ap])


_DBG = {"on": False, "data": None}


def build_program():
    nc = bass.Bass("TRN2", target_bir_lowering=False)
    S = Sched()
    DBG = _DBG["on"]
    L_RUN = 1 if DBG else L

    def din(name, shape, dt=F32):
        return nc.dram_tensor(name, shape, dt, kind="ExternalInput").ap()

    def dscr(name, shape, dt):
        return nc.dram_tensor(name, shape, dt, kind="Internal").ap()

    x_in = din("x", [T, D])
    mem_in = din("mem", [256, D])
    pos_in = din("pos", [1, T], I32)
    vecs_in = din("vecs", [128, L * NVL])
    lam_in = din("lamv", [1, L * 256])
    cst_in = din("cst", [128, 8])
    mask_in = din("masks", [128, 8 * TW])
    mats_in = din("mats", [128, 3 * 128])
    w_in = din("w_in", [L, D, NIN])
    w_uq = din("w_uq", [L, 512, 1536])
    w_ukv = din("w_ukv", [L, 256, 2048])
    w_brd = din("w_br_diff", [L, 1024, D])
    w_brm = din("w_br_mla", [L, 1024, D])
    w_mo = din("w_mix_out", [L, D, D])
    w_qx = din("w_q_x", [L, D, D])
    w_kvx = din("w_kv_x", [L, D, 2 * D])
    w_ox = din("w_o_x", [L, D, D])
    w_up = din("w_up", [L, D, 11264])
    w_dn = din("w_down", [L, 5632, D])
    out_d = nc.dram_tensor("out", [T, D], F32, kind="ExternalOutput").ap()

    dbg_d = nc.dram_tensor("dbg", [3, 128, DC, T], F32, kind="ExternalOutput").ap() if DBG else None
    R_dbg = Res("dbg")
    dbgA_d = nc.dram_tensor("dbgA", [2, 128, DC * T], BF16, kind="ExternalOutput").ap() if DBG else None
    dbg_k = [0]
    xT_d = dscr("xT_d", [128, DC, T], F32)
    yT_d = dscr("yT_d", [128, DC, T], F32)
    qd_d = dscr("qd_d", [8, 128, T], BF16)
    qmn_d = dscr("qmn_d", [8, 128, T], BF16)
    qmp_d = dscr("qmp_d", [4, 128, T], BF16)
    gates_d = dscr("gates_d", [32, 128, T], BF16)
    exin_d = dscr("exin_d", [16, 256, T], BF16)
    exout_d = dscr("exout_d", [16, 512, T], BF16)
    kpe_in_d = dscr("kpe_in_d", [128, T], BF16)
    kpe_out_d = dscr("kpe_out_d", [256, T], BF16)
    halo_in_d = dscr("halo_in_d", [128, 512], BF16)
    halo_out_d = dscr("halo_out_d", [256, 512], BF16)
    rope_d = dscr("rope_d", [2, 128, T], F32)
    memn_d = dscr("memn_d", [128, DC, 256], F32)
    kmT_d = dscr("kmT_d", [L, 128, DC, 256], BF16)
    vm_d = dscr("vm_d", [L, 128, 2, D], BF16)

    R_xT = [Res(f"xT{t}") for t in range(NT)]
    R_yT = [[Res(f"yT{c}_{t}") for t in range(NT)] for c in range(DC)]
    R_qd = [Res(f"qd{h}") for h in range(8)]
    R_qmn = [Res(f"qmn{h}") for h in range(8)]
    R_qmp = [Res(f"qmp{h}") for h in range(4)]
    R_gates = [Res(f"gates{c}") for c in range(32)]
    R_exin = [Res(f"exin{h}") for h in range(16)]
    R_exout = [Res(f"exout{h}") for h in range(16)]
    R_kpein, R_kpeout = Res("kpein"), Res("kpeout")
    R_haloin, R_haloout = Res("haloin"), Res("haloout")
    R_rope = Res("rope")
    R_memn = Res("memn")
    R_kmT = [Res(f"kmT{l}") for l in range(L)]
    R_vm = [Res(f"vm{l}") for l in range(L)]
    R_out = Res("out")

    with contextlib.ExitStack() as es:
        def sb(name, shape, dt):
            return es.enter_context(nc.sbuf_tensor(name, shape, dt))

        A = sb("bigA", [128, DC * T], BF16)
        B32 = sb("bigB", [128, 16384], F32)
        A3 = A[:].rearrange("p (c t) -> p c t", t=T)
        Bb = B32[:].bitcast(BF16)
        Bb3 = Bb.rearrange("p (c t) -> p c t", t=T)
        R_A = [[Res(f"A{c}_{t}") for t in range(NT)] for c in range(DC)]
        R_B = [[Res(f"B{c}_{t}") for t in range(NT)] for c in range(DC)]

        def RA_tt(t):
            return [R_A[c][t] for c in range(DC)]

        def RA_all():
            return [R_A[c][t] for c in range(DC) for t in range(NT)]

        R_Bmisc = []

        def RB_all():
            return [R_B[c][t] for c in range(DC) for t in range(NT)] + R_Bmisc

        WB = [sb(f"wb{i}", [128, 6144], BF16) for i in range(3)]
        R_WB = [Res(f"wb{i}") for i in range(3)]
        vecs = sb("vecs_sb", [128, L * NVL], F32)
        lamt = B32[:, 4096:4096 + L * 256]
        lamw = sb("lamw", [128, 16], F32)
        gsubs = sb("gsubs", [128, L], F32)
        cst = sb("cst_sb", [128, 8], F32)
        masks = Bb[:, 28672:32768]
        R_masks = Res("masks")
        mats = sb("mats_sb", [128, 3 * 128], F32)
        matsb = sb("matsb_sb", [128, 128], BF16)
        ident = mats[:, 0:128]
        perm = mats[:, 128:256]
        ones_bf = matsb[:, :]
        R_const = Res("const")
        R_lamt = Res("lamt")
        R_Bmisc.append(R_lamt)
        NWK = 8
        wk = [sb(f"wk{i}", [128, 520], F32) for i in range(NWK)]
        R_wk = [Res(f"wk{i}") for i in range(NWK)]
        NST = 0
        stg = [sb(f"stg{i}", [128, T], BF16) for i in range(NST)]
        R_stg = [Res(f"stg{i}") for i in range(NST)]
        NPB = 6
        pb = [sb(f"pb{i}", [128, TW], BF16) for i in range(NPB)]
        R_pb = [Res(f"pb{i}") for i in range(NPB)]
        hh = sb("hh_sb", [128, 512], BF16)
        hc = sb("hc_sb", [128, 512], BF16)
        hg = sb("hg_sb", [128, 1024], BF16)
        R_hh, R_hc, R_hg = Res("hh"), Res("hc"), Res("hg")
        posi = sb("posi", [128, TW], I32)
        R_posi = Res("posi")
        ps = [es.enter_context(nc.psum_tensor(f"ps{i}", [128, TW], F32)) for i in range(8)]
        R_ps = [Res(f"ps{i}") for i in range(8)]

        rot = {"wk": 0, "stg": 0, "pb": 0, "wb": 0}

        def nxt(kind, n):
            i = rot[kind]
            rot[kind] = (i + 1) % n
            return i

        def V(l, off, j=0):
            c = l * NVL + off + j
            return vecs[:, c:c + 1]

        S.dma("sp", DMA(vecs[:], vecs_in[:, :]), "ld_const", writes=[R_const])
        S.dma("sp", DMA(cst[:], cst_in[:, :]), "ld_const", writes=[R_const])
        S.dma("sp", DMA(mats[:], mats_in[:, :]), "ld_const", writes=[R_const])
        S.dma("sp", DMA(lamt, bass.AP(lam_in.tensor, 0, [[0, 128], [1, L * 256]])), "ld_const", writes=[R_const, R_lamt])
        S.op("dve", CP(matsb[:, :], mats[:, 256:384]), reads=[R_const], writes=[R_const])
        for l in range(L):
            lam_init = 0.8 - 0.6 * math.exp(-0.3 * l)
            b = l * 256
            S.op("dve", TT(lamt[:, b:b + 64], lamt[:, b:b + 64], lamt[:, b + 64:b + 128], ALU.mult), reads=[R_const, R_lamt], writes=[R_const])
            S.op("dve", TT(lamt[:, b + 128:b + 192], lamt[:, b + 128:b + 192], lamt[:, b + 192:b + 256], ALU.mult), reads=[R_const, R_lamt], writes=[R_const])
            S.op("dve", lambda h, b=b, l=l: h.reduce_sum(out=lamw[:, 4 * l:4 * l + 1], in_=lamt[:, b:b + 64], axis=AX.X), reads=[R_const, R_lamt], writes=[R_const])
            S.op("dve", lambda h, b=b, l=l: h.reduce_sum(out=lamw[:, 4 * l + 1:4 * l + 2], in_=lamt[:, b + 128:b + 192], axis=AX.X), reads=[R_const, R_lamt], writes=[R_const])
            S.op("act", ACT(lamw[:, 4 * l:4 * l + 2], lamw[:, 4 * l:4 * l + 2], AF.Exp), reads=[R_const], writes=[R_const])
            S.op("dve", TT(lamw[:, 4 * l + 2:4 * l + 3], lamw[:, 4 * l + 1:4 * l + 2], lamw[:, 4 * l:4 * l + 1], ALU.subtract), reads=[R_const], writes=[R_const])
            S.op("dve", TS(lamw[:, 4 * l + 3:4 * l + 4], lamw[:, 4 * l + 2:4 * l + 3], -lam_init, None, ALU.add), reads=[R_const], writes=[R_const])
            S.op("dve", TS(gsubs[:, l:l + 1], V(l, V_GSUB), 1.0 - lam_init, None, ALU.mult), reads=[R_const], writes=[R_const])

        def neglam(l):
            return lamw[:, 4 * l + 3:4 * l + 4]

        C1, C2 = 6.28125, 2.0 * math.pi - 6.28125
        MAGIC = 12582912.0

        def wrap_pi(u, t_):
            S.op("dve", TS(wk[t_][:, 0:TW], wk[u][:, 0:TW], math.pi, -2.0 * math.pi, ALU.is_gt, ALU.mult), reads=[R_wk[u]], writes=[R_wk[t_]])
            S.op("dve", TT(wk[u][:, 0:TW], wk[u][:, 0:TW], wk[t_][:, 0:TW], ALU.add), reads=[R_wk[u], R_wk[t_]], writes=[R_wk[u]])
            S.op("dve", TS(wk[t_][:, 0:TW], wk[u][:, 0:TW], -math.pi, 2.0 * math.pi, ALU.is_lt, ALU.mult), reads=[R_wk[u]], writes=[R_wk[t_]])
            S.op("dve", TT(wk[u][:, 0:TW], wk[u][:, 0:TW], wk[t_][:, 0:TW], ALU.add), reads=[R_wk[u], R_wk[t_]], writes=[R_wk[u]])

        for t in range(NT):
            S.dma("sp", DMA(posi[:], bass.AP(pos_in.tensor, t * TW, [[0, 128], [1, TW]])), "ld_posi", writes=[R_posi])
            a0, a1, a2 = 0, 1, 2
            S.op("dve", CP(wk[a0][:, 0:TW], posi[:]), reads=[R_posi], writes=[R_wk[a0]])
            S.op("dve", TS(wk[a0][:, 0:TW], wk[a0][:, 0:TW], cst[:, 0:1], None, ALU.mult), reads=[R_wk[a0], R_const], writes=[R_wk[a0]])
            S.op("dve", TS(wk[a1][:, 0:TW], wk[a0][:, 0:TW], 1.0 / (2.0 * math.pi), MAGIC, ALU.mult, ALU.add), reads=[R_wk[a0]], writes=[R_wk[a1]])
            S.op("dve", TS(wk[a1][:, 0:TW], wk[a1][:, 0:TW], -MAGIC, None, ALU.add), reads=[R_wk[a1]], writes=[R_wk[a1]])
            S.op("dve", STT(wk[a0][:, 0:TW], wk[a1][:, 0:TW], -C1, wk[a0][:, 0:TW], ALU.mult, ALU.add), reads=[R_wk[a0], R_wk[a1]], writes=[R_wk[a0]])
            S.op("dve", STT(wk[a0][:, 0:TW], wk[a1][:, 0:TW], -C2, wk[a0][:, 0:TW], ALU.mult, ALU.add), reads=[R_wk[a0], R_wk[a1]], writes=[R_wk[a0]])
            wrap_pi(a0, a1)
            S.op("act", ACT(wk[a2][:, 0:TW], wk[a0][:, 0:TW], AF.Sin), reads=[R_wk[a0]], writes=[R_wk[a2]])
            S.op("dve", TS(wk[a2][:, 0:TW], wk[a2][:, 0:TW], cst[:, 1:2], None, ALU.mult), reads=[R_wk[a2], R_const], writes=[R_wk[a2]])
            S.dma("sp", DMA(rope_d[1, :, t * TW:(t + 1) * TW], wk[a2][:, 0:TW]), "st_wk2", reads=[R_wk[a2]], writes=[R_rope], partial=True)
            S.op("dve", TS(wk[a0][:, 0:TW], wk[a0][:, 0:TW], 0.5 * math.pi, None, ALU.add), reads=[R_wk[a0]], writes=[R_wk[a0]])
            wrap_pi(a0, a1)
            S.op("act", ACT(wk[3][:, 0:TW], wk[a0][:, 0:TW], AF.Sin), reads=[R_wk[a0]], writes=[R_wk[3]])
            S.dma("sp", DMA(rope_d[0, :, t * TW:(t + 1) * TW], wk[3][:, 0:TW]), "st_wk3", reads=[R_wk[3]], writes=[R_rope], partial=True)

        def rstd_from_bank(bank_i, n, outw):
            S.op("act", ACT(wk[outw][:, 0:TW], ps[bank_i][:], AF.Ln, bias=cst[:, 4:5], scale=1.0 / n), reads=[R_ps[bank_i], R_const], writes=[R_wk[outw]])
            S.op("act", ACT(wk[outw][:, 0:TW], wk[outw][:, 0:TW], AF.Exp, scale=-0.5), reads=[R_wk[outw]], writes=[R_wk[outw]])

        def load_w(W2d, KC, col0, ncols, row0=0, slot_kc0=0, buf=None, tot_kc=None):
            i = nxt("wb", 3) if buf is None else buf
            tot = KC if tot_kc is None else tot_kc
            assert tot * ncols <= 6144
            wv = WB[i][:, 0:tot * ncols].rearrange("p (kc n) -> p kc n", n=ncols)
            src = W2d[row0:row0 + KC * 128, col0:col0 + ncols].rearrange("(kc p) n -> p kc n", p=128)
            S.dma("pool", DMA(wv[:, slot_kc0:slot_kc0 + KC, :], src), f"ld_wb{i}", writes=[R_WB[i]], partial=(slot_kc0 > 0))
            return i, wv

        def gemm_fm(W2d, KC, chunk0, nchunks, act_ap, act_res, consume, banks, row0=0, tt_outer=False, tts=range(NT), head_tt=False):
            deferred = []
            bi = [0]
            nmax = min(4, 6144 // (KC * 128))
            blocks = []
            c = 0
            while c < nchunks:
                nblk = min(nmax, nchunks - c)
                if tt_outer and nchunks == 4:
                    nblk = 2
                blocks.append((c, nblk))
                c += nblk

            def run_tile(c0, oc, tt, i, wv):
                nonlocal deferred
                b = banks[bi[0] % len(banks)]
                bi[0] += 1
                rd = [R_WB[i]] + act_res(tt)
                for kc in range(KC):
                    first, last = kc == 0, kc == KC - 1
                    S.op("pe", MM(ps[b][:], wv[:, kc, oc * 128:(oc + 1) * 128], act_ap(kc, tt), first, last),
                         reads=rd if (first or last) else (), writes=[R_ps[b]])
                prev = deferred
                deferred = []
                for f in prev:
                    f()
                consume(c0 + oc, tt, b, deferred)

            if tt_outer:
                assert len(blocks) <= 2
                loaded = [(c0, nblk) + load_w(W2d, KC, (chunk0 + c0) * 128, nblk * 128, row0=row0) for (c0, nblk) in blocks]
                for tt in tts:
                    for (c0, nblk, i, wv) in loaded:
                        for oc in range(nblk):
                            run_tile(c0, oc, tt, i, wv)
            else:
                rest = blocks
                if head_tt:
                    head, rest = blocks[:2], blocks[2:]
                    loaded = [(c0, nblk) + load_w(W2d, KC, (chunk0 + c0) * 128, nblk * 128, row0=row0) for (c0, nblk) in head]
                    for tt in tts:
                        for (c0, nblk, i, wv) in loaded:
                            for oc in range(nblk):
                                run_tile(c0, oc, tt, i, wv)
                for (c0, nblk) in rest:
                    i, wv = load_w(W2d, KC, (chunk0 + c0) * 128, nblk * 128, row0=row0)
                    for oc in range(nblk):
                        for tt in tts:
                            run_tile(c0, oc, tt, i, wv)
            for f in deferred:
                f()

        def gemm_tm(W2d, KC, col0, ncols, act_blk_ap, act_res_blk, consume, banks, row0=0):
            i, wv = load_w(W2d, KC, col0, ncols, row0=row0)
            for blk in range(NB):
                b = banks[blk % len(banks)]
                rd = [R_WB[i]] + act_res_blk(blk)
                for kc in range(KC):
                    first, last = kc == 0, kc == KC - 1
                    S.op("pe", MM(ps[b][:, 0:ncols], act_blk_ap(kc, blk), wv[:, kc, :], first, last),
                         reads=rd if (first or last) else (), writes=[R_ps[b]])
                consume(blk, b)

        hT_ap = lambda kc, tt: A3[:, kc, tt * TW:(tt + 1) * TW]
        hT_res = lambda tt: RA_tt(tt)

        xn = B32[:, 0:8192].rearrange("p (c t) -> p c t", t=TW)
        yt = B32[:, 8192:16384].rearrange("p (c t) -> p c t", t=TW)
        xin = B32[:, 8192:16384].rearrange("p (b d) -> p b d", d=D)
        R_xn = [[Res(f"xn{c}") for c in range(DC)]]
        R_yt = [Res(f"yt{c}") for c in range(DC)]
        R_Bmisc.extend(R_xn[0] + R_yt)

        def nr_pass(l, mode, g_post_off, g_pre_l, g_pre_off):
            RB = RB_all()
            for tt in range(NT):
                tsl = slice(tt * TW, (tt + 1) * TW)
                if mode == "init":
                    for blk in range(4):
                        r0 = tt * TW + blk * 128
                        S.dma("sp", DMA(xin[:, blk, :], x_in[r0:r0 + 128, :]), "ld_yt", writes=[R_yt[blk]] + (RB if (tt == 0 and blk == 0) else []))
                    for c in range(DC):
                        b = c % 4
                        for blk in range(4):
                            S.op("pe", TR(ps[b][:, blk * 128:(blk + 1) * 128], xin[:, blk, c * 128:(c + 1) * 128], ident),
                                 reads=[R_yt[blk], R_const], writes=[R_ps[b]])
                        S.op("act", ACT(xn[:, c, :], ps[b][:], AF.Copy), reads=[R_ps[b]], writes=[R_xn[0][c]] + (RB if (tt == 0 and c == 0) else []))
                else:
                    for c in range(DC):
                        S.dma("sp", DMA(yt[:, c, :], yT_d[:, c, tsl]), "ld_yt", reads=[R_yT[c][tt]], writes=[R_yt[c]] + (RB if (tt == 0 and c == 0) else []))
                    for c in range(DC):
                        p = nxt("pb", NPB)
                        S.op("act", ACT(pb[p][:], yt[:, c, :], AF.Square), reads=[R_yt[c]], writes=[R_pb[p]])
                        S.op("pe", MM(ps[4][:], ones_bf, pb[p][:], c == 0, c == DC - 1), reads=[R_pb[p], R_const], writes=[R_ps[4]])
                    rstd_from_bank(4, D, 6)
                    for c in range(DC):
                        w = nxt("wk", 4)
                        S.dma("sp", DMA(wk[w][:, 0:TW], xT_d[:, c, tsl]), f"ld_wk{w}", reads=[R_xT[tt]], writes=[R_wk[w]])
                        S.op("dve", STT(yt[:, c, :], yt[:, c, :], V(l, g_post_off, c), wk[6][:, 0:TW], ALU.mult, ALU.mult),
                             reads=[R_yt[c], R_wk[6], R_const], writes=[R_yt[c]])
                        S.op("pool", TT(xn[:, c, :], yt[:, c, :], wk[w][:, 0:TW], ALU.add), reads=[R_yt[c], R_wk[w]],
                             writes=[R_xn[0][c]] + (RB if (tt == 0 and c == 0) else []))
                if mode != "final":
                    for c in range(DC):
                        S.dma("sp", DMA(xT_d[:, c, tsl], xn[:, c, :]), "st_xn", reads=[R_xn[0][c]], writes=[R_xT[tt]], partial=(c > 0))
                        if DBG and mode == "mid":
                            S.dma("sp", DMA(dbg_d[dbg_k[0], :, c, tsl], xn[:, c, :]), "st_xn", reads=[R_xn[0][c]], writes=[R_dbg], partial=True)
                        p = nxt("pb", NPB)
                        S.op("act", ACT(pb[p][:], xn[:, c, :], AF.Square), reads=[R_xn[0][c]], writes=[R_pb[p]])
                        S.op("pe", MM(ps[5][:], ones_bf, pb[p][:], c == 0, c == DC - 1), reads=[R_pb[p], R_const], writes=[R_ps[5]])
                    rstd_from_bank(5, D, 7)
                    for c in range(DC):
                        S.op("dve", STT(A3[:, c, tsl], xn[:, c, :], V(g_pre_l, g_pre_off, c), wk[7][:, 0:TW], ALU.mult, ALU.mult),
                             reads=[R_xn[0][c], R_wk[7], R_const], writes=[R_A[c][tt]])
                    if DBG and mode == "mid" and tt == NT - 1:
                        dbg_k[0] += 1
                else:
                    for blk in range(4):
                        ot = yt
                        otile = B32[:, 8192 + (blk % 2) * 2048: 8192 + (blk % 2) * 2048 + 2048]
                        for c in range(DC):
                            b = c // 4
                            S.op("pe", TR(ps[b][:, (c % 4) * 128:(c % 4 + 1) * 128], xn[:, c, blk * 128:(blk + 1) * 128], ident),
                                 reads=[R_xn[0][c], R_const], writes=[R_ps[b]])
                        for b in range(4):
                            S.op("act" if b % 2 == 0 else "dve", (ACT(otile[:, b * 512:(b + 1) * 512], ps[b][:], AF.Copy) if b % 2 == 0 else CP(otile[:, b * 512:(b + 1) * 512], ps[b][:])),
                                 reads=[R_ps[b]], writes=[R_yt[(blk % 2) * 4 + b]])
                        r0 = tt * TW + blk * 128
                        S.dma("sp", DMA(out_d[r0:r0 + 128, :], otile), "st_out", reads=[R_yt[(blk % 2) * 4 + b] for b in range(4)], writes=[R_out], partial=True)

        def consume_store_bf16(dst_fn, res_fn, func=AF.Copy, bias_fn=None):
            seen = set()

            def consume(oc, tt, b, deferred):
                p = nxt("pb", NPB)
                kw = {} if bias_fn is None else {"bias": bias_fn(oc)}
                S.op("act", ACT(pb[p][:], ps[b][:], func, **kw), reads=[R_ps[b], R_const], writes=[R_pb[p]])
                S.dma("sp", DMA(dst_fn(oc)[:, tt * TW:(tt + 1) * TW], pb[p][:]), f"st_pb{p}", reads=[R_pb[p]], writes=[res_fn(oc)], partial=(oc in seen))
                seen.add(oc)
            return consume

        ropeC = B32[:, 0:2048]
        ropeS = B32[:, 2048:4096]
        R_ropes = Res("ropes")
        R_Bmisc.append(R_ropes)

        def consume_rope(dst_fn, res_fn):
            seen = set()

            def consume(oc, tt, b, deferred):
                q = nxt("wk", 4)
                S.op("act", ACT(wk[q][:, 0:TW], ps[b][:], AF.Copy), reads=[R_ps[b]], writes=[R_wk[q]])

                def later():
                    pbk = 6 + (q % 2)
                    S.op("pe", MM(ps[pbk][:], perm, wk[q][:, 0:TW], True, True), reads=[R_wk[q], R_const], writes=[R_ps[pbk]])
                    t2 = 4 + (q % 2)
                    S.op("dve", TT(wk[t2][:, 0:TW], ps[pbk][:], ropeS[:, tt * TW:(tt + 1) * TW], ALU.mult), reads=[R_ps[pbk], R_ropes], writes=[R_wk[t2]])
                    S.op("dve", TT(wk[q][:, 0:TW], wk[q][:, 0:TW], ropeC[:, tt * TW:(tt + 1) * TW], ALU.mult), reads=[R_wk[q], R_ropes], writes=[R_wk[q]])
                    p = nxt("pb", NPB)
                    S.op("dve", TT(pb[p][:], wk[q][:, 0:TW], wk[t2][:, 0:TW], ALU.add), reads=[R_wk[q], R_wk[t2]], writes=[R_pb[p]])
                    S.dma("sp", DMA(dst_fn(oc)[:, tt * TW:(tt + 1) * TW], pb[p][:]), f"st_pb{p}", reads=[R_pb[p]], writes=[res_fn(oc)], partial=(oc in seen))
                    seen.add(oc)
                deferred.append(later)
            return consume

        def consume_tokmajor_V(h0, nh):
            def consume(blk, b):
                p = nxt("pb", NPB)
                S.op("act", ACT(pb[p][:, 0:nh * 128], ps[b][:, 0:nh * 128], AF.Copy), reads=[R_ps[b]], writes=[R_pb[p]])
                dst = exin_d[h0:h0 + nh, 128:256, blk * 128:(blk + 1) * 128].rearrange("h p d -> p h d")
                S.dma("sp", DMA(dst, pb[p][:, 0:nh * 128].rearrange("p (h d) -> p h d", d=128)), f"st_pb{p}", reads=[R_pb[p]],
                      writes=[R_exin[h0 + j] for j in range(nh)], partial=True)
            return consume

        cqn = Bb[:, 8192:8192 + 4 * T].rearrange("p (c t) -> p c t", t=T)
        ckvn = Bb[:, 8192 + 4 * T:8192 + 6 * T].rearrange("p (c t) -> p c t", t=T)
        R_cqn = [Res(f"cqn{t}") for t in range(NT)]
        R_ckvn = [Res(f"ckvn{t}") for t in range(NT)]
        R_Bmisc.extend(R_cqn + R_ckvn)

        def consume_latent(l, nck, dstv, dres, goff, bank_ssq):
            def consume(oc, tt, b, deferred):
                w = oc
                S.op("act", ACT(wk[w][:, 0:TW], ps[b][:], AF.Copy), reads=[R_ps[b]], writes=[R_wk[w]])
                p = nxt("pb", NPB)
                S.op("act", ACT(pb[p][:], wk[w][:, 0:TW], AF.Square), reads=[R_wk[w]], writes=[R_pb[p]])

                def later():
                    S.op("pe", MM(ps[bank_ssq][:], ones_bf, pb[p][:], oc == 0, oc == nck - 1), reads=[R_pb[p], R_const], writes=[R_ps[bank_ssq]])
                    if oc == nck - 1:
                        rstd_from_bank(bank_ssq, nck * 128, 5)
                        for c2 in range(nck):
                            S.op("dve", STT(dstv[:, c2, tt * TW:(tt + 1) * TW], wk[c2][:, 0:TW], V(l, goff, c2), wk[5][:, 0:TW], ALU.mult, ALU.mult),
                                 reads=[R_wk[c2], R_wk[5], R_const], writes=[dres[tt]])
                deferred.append(later)
            return consume

        def attn_keytiles(tt):
            tiles = []
            for r in range(2):
                for kb in range(4 * tt + 4):
                    tiles.append((r, kb, (r * 4 + kb - 4 * tt) if kb >= 4 * tt else None))
            return tiles

        def hb(slot):
            base = slot * 12288
            q = Bb[:, base:base + 2048]
            q2 = Bb[:, base + 2048:base + 4096]
            k = Bb[:, base + 4096:base + 8192].rearrange("p (r t) -> p r t", t=T)
            v = Bb[:, base + 8192:base + 12288].rearrange("p (s d) -> p s d", d=128)
            return q, q2, k, v
        kpe_sb = Bb[:, 24576:24576 + 4096].rearrange("p (r t) -> p r t", t=T)
        R_hb = [Res("hb0"), Res("hb1")]
        R_kpe = Res("kpe_sb")
        R_Bmisc.extend(R_hb + [R_kpe, R_masks])

        def load_head(slot, h, kind):
            q, q2, k, v = hb(slot)
            if kind == "diff":
                S.dma("sp", DMA(q, qd_d[h]), f"ld_hb{slot}", reads=[R_qd[h]], writes=[R_hb[slot]])
                e = h
            else:
                S.dma("sp", DMA(q, qmn_d[h]), f"ld_hb{slot}", reads=[R_qmn[h]], writes=[R_hb[slot]])
                S.dma("sp", DMA(q2, qmp_d[h // 2]), f"ld_hb{slot}", reads=[R_qmp[h // 2]], writes=[R_hb[slot]], partial=True)
                e = 8 + h
            for r in range(2):
                S.dma("sp", DMA(k[:, r, :], exout_d[e, r * 256:r * 256 + 128, :]), f"ld_hb{slot}", reads=[R_exout[e]], writes=[R_hb[slot]], partial=True)
                S.dma("sp", DMA(v[:, r * 16:(r + 1) * 16, :], exout_d[e, r * 256 + 128:r * 256 + 256, :].rearrange("p (b d) -> p b d", d=128)),
                      f"ld_hb{slot}", reads=[R_exout[e]], writes=[R_hb[slot]], partial=True)

        acc = [sb(f"acc{i}", [128, TW], F32) for i in range(2)]
        R_acc = [Res("acc0"), Res("acc1")]
        sqt = sb("sqt", [128, TW], BF16)
        R_sqt = Res("sqt")
        ones_f = mats[:, 256:384]
        qt_ctr = [0]
        pending = []

        def run_pending(item_no, force=False):
            keep = []
            for (due, fn) in pending:
                if force or item_no >= due:
                    fn(item_no)
                else:
                    keep.append((due, fn))
            pending[:] = keep

        def diff_head(l, h, slot):
            q, q2, k, v = hb(slot)
            scale = 1.0 / math.sqrt(64.0)
            items = [(tt, i) for tt in range(NT) for i in range(len(attn_keytiles(tt)))]

            def scores(j):
                tt, i = items[j]
                r, kb, mid = attn_keytiles(tt)[i]
                sb_ = (j % 2) * 2
                for half in range(2):
                    pr = slice(half * 64, half * 64 + 64)
                    S.op("pe", MM(ps[sb_ + half][:], k[pr, r, kb * 128:(kb + 1) * 128], q[pr, tt * TW:(tt + 1) * TW], True, True),
                         reads=[R_hb[slot]], writes=[R_ps[sb_ + half]])
            scores(0)
            for j, (tt, i) in enumerate(items):
                tsl = slice(tt * TW, (tt + 1) * TW)
                tiles = attn_keytiles(tt)
                n = len(tiles)
                r, kb, mid = tiles[i]
                sb_ = (j % 2) * 2
                if j + 1 < len(items):
                    scores(j + 1)
                for half in range(2):
                    p = nxt("pb", NPB)
                    src = ps[sb_ + half][:]
                    rd = [R_ps[sb_ + half]]
                    if mid is not None:
                        w = nxt("wk", 4)
                        S.op("dve", TT(wk[w][:, 0:TW], ps[sb_ + half][:], masks[:, mid * TW:(mid + 1) * TW], ALU.add),
                             reads=[R_ps[sb_ + half], R_masks], writes=[R_wk[w]])
                        src = wk[w][:, 0:TW]
                        rd = [R_wk[w]]
                    S.op("act", ACT(pb[p][:], src, AF.Exp, scale=scale), reads=rd, writes=[R_pb[p]])
                    S.op("pe", MM(ps[4 + half][:], v[:, r * 16 + kb, :], pb[p][:], i == 0, i == n - 1), reads=[R_pb[p], R_hb[slot]], writes=[R_ps[4 + half]])
                    S.op("pe", MM(ps[6 + half][:], ones_bf, pb[p][:], i == 0, i == n - 1), reads=[R_pb[p], R_const], writes=[R_ps[6 + half]])
                run_pending(j)
                if i == n - 1:
                    S.op("dve", CP(acc[0][:], ps[6][:]), reads=[R_ps[6]], writes=[R_acc[0]])
                    S.op("dve", CP(acc[1][:], ps[7][:]), reads=[R_ps[7]], writes=[R_acc[1]])
                    S.op("dve", CP(wk[4][:, 0:TW], ps[4][:]), reads=[R_ps[4]], writes=[R_wk[4]])
                    S.op("dve", CP(wk[5][:, 0:TW], ps[5][:]), reads=[R_ps[5]], writes=[R_wk[5]])

                    def st_a(item_no):
                        for a_ in range(2):
                            S.op("act", ACT(acc[a_][:], acc[a_][:], AF.Ln), reads=[R_acc[a_]], writes=[R_acc[a_]])
                            S.op("act", ACT(acc[a_][:], acc[a_][:], AF.Exp, scale=-1.0), reads=[R_acc[a_]], writes=[R_acc[a_]])

                    def st_b(item_no, l=l):
                        S.op("dve", TT(wk[4][:, 0:TW], wk[4][:, 0:TW], acc[0][:], ALU.mult), reads=[R_wk[4], R_acc[0]], writes=[R_wk[4]])
                        S.op("dve", TT(wk[5][:, 0:TW], wk[5][:, 0:TW], acc[1][:], ALU.mult), reads=[R_wk[5], R_acc[1]], writes=[R_wk[5]])
                        S.op("dve", STT(wk[4][:, 0:TW], wk[5][:, 0:TW], neglam(l), wk[4][:, 0:TW], ALU.mult, ALU.add), reads=[R_wk[4], R_wk[5], R_const], writes=[R_wk[4]])

                    def st_c(item_no):
                        S.op("act", ACT(sqt[:], wk[4][:, 0:TW], AF.Square), reads=[R_wk[4]], writes=[R_sqt])

                    def st_d(item_no, h=h, tsl=tsl, tt=tt, l=l):
                        bk = (item_no % 2) * 2
                        S.op("pe", MM(ps[bk][:], ones_bf, sqt[:], True, True), reads=[R_sqt, R_const], writes=[R_ps[bk]])
                        rstd_from_bank(bk, 128, 5)
                        S.op("dve", STT(A3[:, h, tsl], wk[4][:, 0:TW], gsubs[:, l:l + 1], wk[5][:, 0:TW], ALU.mult, ALU.mult),
                             reads=[R_wk[4], R_wk[5], R_const], writes=[R_A[h][tt]])
                    if j + 7 < len(items):
                        pending.append((j + 2, st_a))
                        pending.append((j + 3, st_b))
                        pending.append((j + 4, st_c))
                        pending.append((j + 6, st_d))
                    else:
                        st_a(j); st_b(j); st_c(j); st_d(j)
            run_pending(len(items), force=True)

        def mla_head(l, h, slot):
            q, q2, k, v = hb(slot)
            scale = 1.0 / math.sqrt(192.0)
            pr = slice((h % 2) * 64, (h % 2) * 64 + 64)
            items = [(tt, i) for tt in range(NT) for i in range(len(attn_keytiles(tt)))]
            NI = len(items)

            def scores(j):
                tt, i = items[j]
                r, kb, mid = attn_keytiles(tt)[i]
                sb_ = j % 4
                tsl = slice(tt * TW, (tt + 1) * TW)
                S.op("pe", MM(ps[sb_][:], k[:, r, kb * 128:(kb + 1) * 128], q[:, tsl], True, False), reads=[R_hb[slot]], writes=[R_ps[sb_]])
                S.op("pe", MM(ps[sb_][:], kpe_sb[pr, r, kb * 128:(kb + 1) * 128], q2[pr, tsl], False, True), reads=[R_hb[slot], R_kpe], writes=[R_ps[sb_]])
            scores(0)
            scores(1)
            for j, (tt, i) in enumerate(items):
                tsl = slice(tt * TW, (tt + 1) * TW)
                tiles = attn_keytiles(tt)
                n = len(tiles)
                r, kb, mid = tiles[i]
                sb_ = j % 4
                ob = 4 + (qt_ctr[0] % 2)
                lb = 6 + (qt_ctr[0] % 2)
                if j + 2 < NI:
                    scores(j + 2)
                p = nxt("pb", NPB)
                src = ps[sb_][:]
                rd = [R_ps[sb_]]
                if mid is not None:
                    w = nxt("wk", 4)
                    S.op("dve", TT(wk[w][:, 0:TW], ps[sb_][:], masks[:, mid * TW:(mid + 1) * TW], ALU.add), reads=[R_ps[sb_], R_masks], writes=[R_wk[w]])
                    src = wk[w][:, 0:TW]
                    rd = [R_wk[w]]
                S.op("act", ACT(pb[p][:], src, AF.Exp, scale=scale), reads=rd, writes=[R_pb[p]])
                S.op("pe", MM(ps[ob][:], v[:, r * 16 + kb, :], pb[p][:], i == 0, i == n - 1), reads=[R_pb[p], R_hb[slot]], writes=[R_ps[ob]])
                S.op("pe", MM(ps[lb][:], ones_bf, pb[p][:], i == 0, i == n - 1), reads=[R_pb[p], R_const], writes=[R_ps[lb]])
                run_pending(j)
                if i == n - 1:
                    w4 = 4 + (qt_ctr[0] % 2)

                    def m_a(item_no, w4=w4, lb=lb):
                        S.op("act", ACT(wk[w4][:, 0:TW], ps[lb][:], AF.Ln), reads=[R_ps[lb]], writes=[R_wk[w4]])
                        S.op("act", ACT(wk[w4][:, 0:TW], wk[w4][:, 0:TW], AF.Exp, scale=-1.0), reads=[R_wk[w4]], writes=[R_wk[w4]])

                    def m_b(item_no, w4=w4, ob=ob, h=h, tsl=tsl, tt=tt):
                        S.op("dve", TT(A3[:, 8 + h, tsl], ps[ob][:], wk[w4][:, 0:TW], ALU.mult), reads=[R_ps[ob], R_wk[w4]], writes=[R_A[8 + h][tt]])
                    if j + 3 < NI:
                        pending.append((j + 1, m_a))
                        pending.append((j + 2, m_b))
                    else:
                        m_a(j); m_b(j)
                    qt_ctr[0] += 1
            run_pending(NI, force=True)

        R_memx = [Res("memx0"), Res("memx1")]
        R_kst, R_vst = Res("kst"), Res("vst")
        memx = [B32[:, 8192:8192 + 2048], B32[:, 8192 + 2048:8192 + 4096]]
        mnT = B32[:, 0:4096].rearrange("p (c t) -> p c t", t=256)
        R_mnT = Res("mnT")
        R_Bmisc.extend(R_memx + [R_kst, R_vst, R_mnT])
        for blk in range(2):
            S.dma("sp", DMA(memx[blk], mem_in[blk * 128:(blk + 1) * 128, :]), "ld_yt", writes=[R_memx[blk]])
        for c in range(DC):
            b = c % 4
            for blk in range(2):
                S.op("pe", TR(ps[b][:, blk * 128:(blk + 1) * 128], memx[blk][:, c * 128:(c + 1) * 128], ident), reads=[R_memx[blk], R_const], writes=[R_ps[b]])
            S.op("act", ACT(mnT[:, c, :], ps[b][:, 0:256], AF.Copy), reads=[R_ps[b]], writes=[R_mnT])
            p = nxt("pb", NPB)
            S.op("act", ACT(pb[p][:, 0:256], mnT[:, c, :], AF.Square), reads=[R_mnT], writes=[R_pb[p]])
            S.op("pe", MM(ps[5][:, 0:256], ones_bf, pb[p][:, 0:256], c == 0, c == DC - 1), reads=[R_pb[p], R_const], writes=[R_ps[5]])
        S.op("act", ACT(wk[7][:, 0:256], ps[5][:, 0:256], AF.Sqrt, bias=cst[:, 4:5], scale=1.0 / D), reads=[R_ps[5], R_const], writes=[R_wk[7]])
        S.op("dve", RCP(wk[7][:, 0:256], wk[7][:, 0:256]), reads=[R_wk[7]], writes=[R_wk[7]])
        mT = Bb[:, 16384:16384 + DC * 256].rearrange("p (c t) -> p c t", t=256)
        R_mT = Res("mT")
        R_Bmisc.append(R_mT)
        for l in range(L):
            for c in range(DC):
                S.op("dve", STT(mT[:, c, :], mnT[:, c, :], V(l, V_GMEM, c), wk[7][:, 0:256], ALU.mult, ALU.mult), reads=[R_mnT, R_wk[7], R_const], writes=[R_mT])
            kst = Bb[:, 24576:24576 + DC * 256].rearrange("p (c t) -> p c t", t=256)
            for cb in range(8):
                i, wv = load_w(w_kvx[l], DC, cb * 256, 256)
                for oc in range(2):
                    b = (cb * 2 + oc) % 4
                    for kc in range(DC):
                        S.op("pe", MM(ps[b][:, 0:256], wv[:, kc, oc * 128:(oc + 1) * 128], mT[:, kc, :], kc == 0, kc == DC - 1),
                             reads=[R_WB[i], R_mT] if kc in (0, DC - 1) else (), writes=[R_ps[b]])
                    S.op("act", ACT(kst[:, cb * 2 + oc, :], ps[b][:, 0:256], AF.Copy), reads=[R_ps[b]], writes=[R_kst])
            S.dma("sp", DMA(kmT_d[l], kst), "st_kst", reads=[R_kst], writes=[R_kmT[l]])
            vst = Bb[:, 28672:28672 + 2 * D].rearrange("p (b d) -> p b d", d=D)
            for nb4 in range(8):
                i, wv = load_w(w_kvx[l], DC, D + nb4 * 256, 256)
                for blk in range(2):
                    b = 4 + blk
                    for kc in range(DC):
                        S.op("pe", MM(ps[b][:, 0:256], mT[:, kc, blk * 128:(blk + 1) * 128], wv[:, kc, :], kc == 0, kc == DC - 1),
                             reads=[R_WB[i], R_mT] if kc in (0, DC - 1) else (), writes=[R_ps[b]])
                    S.op("act", ACT(vst[:, blk, nb4 * 256:(nb4 + 1) * 256], ps[b][:, 0:256], AF.Copy), reads=[R_ps[b]], writes=[R_vst])
            S.dma("sp", DMA(vm_d[l], vst), "st_vst", reads=[R_vst], writes=[R_vm[l]])

        nr_pass(0, "init", 0, 0, V_GPM)
        for l in range(L_RUN):
            S.dma("sp", DMA(B32[:, 0:2048], rope_d[0]), "ld_ropes", reads=[R_rope], writes=[R_ropes] + RB_all())
            S.dma("sp", DMA(B32[:, 2048:4096], rope_d[1]), "ld_ropes", reads=[R_rope], writes=[R_ropes], partial=True)
            W = w_in[l]
            gb = [0, 1, 2, 3]
            gemm_fm(W, DC, 0, 8, hT_ap, hT_res, consume_rope(lambda oc: exin_d[oc, 0:128, :], lambda oc: R_exin[oc]), gb, head_tt=True)
            for g in range(4):
                gemm_tm(W, DC, 1024 + g * 256, 256, lambda kc, blk: A3[:, kc, blk * 128:(blk + 1) * 128], lambda blk: RA_tt(blk // 4),
                        consume_tokmajor_V(g * 2, 2), [4, 5])
            for h in range(8):
                S.dma("pool", CC(exin_d[h], exout_d[h]), "cc", reads=[R_exin[h]], writes=[R_exout[h]], inc=1)
            gemm_fm(W, DC, 16, 2, hT_ap, hT_res, consume_latent(l, 2, ckvn, R_ckvn, V_GCKV, 5), gb, tt_outer=True)
            gemm_fm(W, DC, 18, 1, hT_ap, hT_res, consume_rope(lambda oc: kpe_in_d[:, :], lambda oc: R_kpein), gb)
            S.dma("pool", CC(kpe_in_d[:, :], kpe_out_d[:, :]), "cc", reads=[R_kpein], writes=[R_kpeout], inc=1)
            ckv_ap = lambda kc, tt: ckvn[:, kc, tt * TW:(tt + 1) * TW]
            ckv_res = lambda tt: [R_ckvn[tt]]
            gemm_fm(w_ukv[l], 2, 0, 8, ckv_ap, ckv_res, consume_store_bf16(lambda oc: exin_d[8 + oc, 0:128, :], lambda oc: R_exin[8 + oc]), gb)
            for g in range(2):
                gemm_tm(w_ukv[l], 2, 1024 + g * 512, 512, lambda kc, blk: ckvn[:, kc, blk * 128:(blk + 1) * 128], lambda blk: [R_ckvn[blk // 4]],
                        consume_tokmajor_V(8 + g * 4, 4), [4, 5])
            for h in range(8):
                S.dma("pool", CC(exin_d[8 + h], exout_d[8 + h]), "cc", reads=[R_exin[8 + h]], writes=[R_exout[8 + h]], inc=1)
            gemm_fm(W, DC, 19, 8, hT_ap, hT_res, consume_rope(lambda oc: qd_d[oc], lambda oc: R_qd[oc]), gb)
            gemm_fm(W, DC, 27, 4, hT_ap, hT_res, consume_latent(l, 4, cqn, R_cqn, V_GCQ, 5), gb, tt_outer=True)
            cq_ap = lambda kc, tt: cqn[:, kc, tt * TW:(tt + 1) * TW]
            cq_res = lambda tt: [R_cqn[tt]]
            gemm_fm(w_uq[l], 4, 0, 8, cq_ap, cq_res, consume_store_bf16(lambda oc: qmn_d[oc], lambda oc: R_qmn[oc]), gb)
            gemm_fm(w_uq[l], 4, 8, 4, cq_ap, cq_res, consume_rope(lambda oc: qmp_d[oc], lambda oc: R_qmp[oc]), gb)
            gemm_fm(W, DC, 31, 32, hT_ap, hT_res,
                    consume_store_bf16(lambda oc: gates_d[oc], lambda oc: R_gates[oc], func=AF.Sigmoid, bias_fn=lambda oc: V(l, V_BG, oc)), gb)

            S.dma("sp", DMA(kpe_sb[:, 0, :], kpe_out_d[0:128, :]), "ld_kpe", reads=[R_kpeout], writes=[R_kpe] + RB_all())
            S.dma("sp", DMA(kpe_sb[:, 1, :], kpe_out_d[128:256, :]), "ld_kpe", reads=[R_kpeout], writes=[R_kpe], partial=True)
            S.dma("pool", DMA(masks, mask_in[:, :]), "ld_masks", writes=[R_masks])
            heads = [("diff", h) for h in range(8)] + [("mla", h) for h in range(8)]
            load_head(0, heads[0][1], heads[0][0])
            for hi, (kind, h) in enumerate(heads):
                slot = hi % 2
                if hi + 1 < len(heads):
                    load_head(1 - slot, heads[hi + 1][1], heads[hi + 1][0])
                if kind == "diff":
                    diff_head(l, h, slot)
                else:
                    mla_head(l, h, slot)

            if DBG:
                S.dma("sp", DMA(dbgA_d[0], A[:]), "st_dbgA", reads=RA_all(), writes=[R_dbg], partial=True)
            first_b = True
            p4_tiles = [(c, tt) for c in range(DC) for tt in range(NT)]
            gate_tiles = {}

            def load_gates(idx):
                if idx >= len(p4_tiles) or idx in gate_tiles:
                    return
                c, tt = p4_tiles[idx]
                tsl = slice(tt * TW, (tt + 1) * TW)
                ga, gbt = nxt("pb", NPB), nxt("pb", NPB)
                S.dma("sp", DMA(pb[ga][:], gates_d[c, :, tsl]), f"ld_pb{ga}", reads=[R_gates[c]], writes=[R_pb[ga]])
                S.dma("sp", DMA(pb[gbt][:], gates_d[16 + c, :, tsl]), f"ld_pb{gbt}", reads=[R_gates[16 + c]], writes=[R_pb[gbt]])
                gate_tiles[idx] = (ga, gbt)
            load_gates(0)
            load_gates(1)
            c0 = 0
            tix = 0
            while c0 < DC:
                nblk = min(3, DC - c0)
                i = nxt("wb", 3)
                load_w(w_brd[l], 8, c0 * 128, nblk * 128, slot_kc0=0, buf=i, tot_kc=16)
                _, wv = load_w(w_brm[l], 8, c0 * 128, nblk * 128, slot_kc0=8, buf=i, tot_kc=16)
                for oc in range(nblk):
                    c = c0 + oc
                    for tt in range(NT):
                        tsl = slice(tt * TW, (tt + 1) * TW)
                        ba, bb = [(0, 1), (2, 3), (4, 5), (6, 7)][tix % 4]
                        load_gates(tix + 2)
                        for kc in range(8):
                            S.op("pe", MM(ps[ba][:], wv[:, kc, oc * 128:(oc + 1) * 128], A3[:, kc, tsl], kc == 0, kc == 7),
                                 reads=([R_WB[i]] + [R_A[k][tt] for k in range(8)]) if kc in (0, 7) else (), writes=[R_ps[ba]])
                        for kc in range(8):
                            S.op("pe", MM(ps[bb][:], wv[:, 8 + kc, oc * 128:(oc + 1) * 128], A3[:, 8 + kc, tsl], kc == 0, kc == 7),
                                 reads=([R_WB[i]] + [R_A[8 + k][tt] for k in range(8)]) if kc in (0, 7) else (), writes=[R_ps[bb]])
                        ga, gbt = gate_tiles.pop(tix)
                        w = nxt("wk", 4)
                        S.op("dve", TT(wk[w][:, 0:TW], ps[ba][:], pb[ga][:], ALU.mult), reads=[R_ps[ba], R_pb[ga]], writes=[R_wk[w]])
                        w2 = 4 + (w % 2)
                        S.op("dve", TT(wk[w2][:, 0:TW], ps[bb][:], pb[gbt][:], ALU.mult), reads=[R_ps[bb], R_pb[gbt]], writes=[R_wk[w2]])
                        S.op("dve", TT(Bb3[:, c, tsl], wk[w][:, 0:TW], wk[w2][:, 0:TW], ALU.add), reads=[R_wk[w], R_wk[w2]],
                             writes=[R_B[c][tt]] + (RB_all() if first_b else []))
                        first_b = False
                        tix += 1
                c0 += nblk

            def consume_y(res_c_off=0):
                def consume(oc, tt, b, deferred):
                    w = nxt("wk", 4)
                    S.op("act", ACT(wk[w][:, 0:TW], ps[b][:], AF.Copy), reads=[R_ps[b]], writes=[R_wk[w]])
                    S.dma("sp", DMA(yT_d[:, oc, tt * TW:(tt + 1) * TW], wk[w][:, 0:TW]), f"st_wk{w}", reads=[R_wk[w]], writes=[R_yT[oc][tt]])
                return consume
            B_ap = lambda kc, tt: Bb3[:, kc, tt * TW:(tt + 1) * TW]
            B_res = lambda tt: [R_B[c][tt] for c in range(DC)]
            gemm_fm(w_mo[l], DC, 0, DC, B_ap, B_res, consume_y(), [0, 1, 2, 3, 4, 5, 6, 7])
            nr_pass(l, "mid", V_GPOM, l, V_GPX)

            def consume_qx(oc, tt, b, deferred):
                S.op("act", ACT(Bb3[:, oc, tt * TW:(tt + 1) * TW], ps[b][:], AF.Copy), reads=[R_ps[b]],
                     writes=[R_B[oc][tt]] + (RB_all() if (oc == 0 and tt == 0) else []))
            gemm_fm(w_qx[l], DC, 0, DC, hT_ap, hT_res, consume_qx, [0, 1, 2, 3, 4, 5, 6, 7], head_tt=True)
            kmv = WB[0][:, 0:DC * 256].rearrange("p (c t) -> p c t", t=256)
            vmv = WB[1][:, 0:2 * D].rearrange("p (b d) -> p b d", d=D)
            S.dma("sp", DMA(kmv, kmT_d[l]), "ld_wb0", reads=[R_kmT[l]], writes=[R_WB[0]])
            S.dma("sp", DMA(vmv, vm_d[l]), "ld_wb1", reads=[R_vm[l]], writes=[R_WB[1]])
            xscale = 1.0 / math.sqrt(512.0)
            steps = [(tt, hx) for tt in range(NT) for hx in range(4)]
            ptiles = {}

            def x_scores(g):
                tt, hx = steps[g]
                tsl = slice(tt * TW, (tt + 1) * TW)
                sbase = 0 if g % 2 == 0 else 6
                pts = []
                for kt in range(2):
                    sbk = sbase + kt
                    for c4 in range(4):
                        c = hx * 4 + c4
                        S.op("pe", MM(ps[sbk][:], kmv[:, c, kt * 128:(kt + 1) * 128], Bb3[:, c, tsl], c4 == 0, c4 == 3),
                             reads=[R_WB[0], R_B[c][tt]], writes=[R_ps[sbk]])
                    p = nxt("pb", NPB)
                    S.op("act", ACT(pb[p][:], ps[sbk][:], AF.Exp, scale=xscale), reads=[R_ps[sbk]], writes=[R_pb[p]])
                    pts.append(p)
                ptiles[g] = pts
            x_scores(0)
            for g, (tt, hx) in enumerate(steps):
                tsl = slice(tt * TW, (tt + 1) * TW)
                if g + 1 < len(steps):
                    x_scores(g + 1)
                pts = ptiles.pop(g)
                lbk = 0 if g % 2 == 0 else 6
                for kt in range(2):
                    S.op("pe", MM(ps[lbk][:], ones_bf, pb[pts[kt]][:], kt == 0, kt == 1), reads=[R_pb[pts[kt]], R_const], writes=[R_ps[lbk]])
                for c4 in range(4):
                    c = hx * 4 + c4
                    for kt in range(2):
                        S.op("pe", MM(ps[2 + c4][:], vmv[:, kt, c * 128:(c + 1) * 128], pb[pts[kt]][:], kt == 0, kt == 1),
                             reads=[R_pb[pts[kt]], R_WB[1]], writes=[R_ps[2 + c4]])
                w = 4 + (g % 2)
                S.op("act", ACT(wk[w][:, 0:TW], ps[lbk][:], AF.Ln), reads=[R_ps[lbk]], writes=[R_wk[w]])
                S.op("act", ACT(wk[w][:, 0:TW], wk[w][:, 0:TW], AF.Exp, scale=-1.0), reads=[R_wk[w]], writes=[R_wk[w]])
                for c4 in range(4):
                    c = hx * 4 + c4
                    S.op("dve", TT(A3[:, c, tsl], ps[2 + c4][:], wk[w][:, 0:TW], ALU.mult), reads=[R_ps[2 + c4], R_wk[w]], writes=[R_A[c][tt]])

            if DBG:
                S.dma("sp", DMA(dbgA_d[1], A[:]), "st_dbgA", reads=RA_all(), writes=[R_dbg], partial=True)
            gemm_fm(w_ox[l], DC, 0, DC, hT_ap, hT_res, consume_y(), [0, 1, 2, 3, 4, 5, 6, 7])
            nr_pass(l, "mid", V_GPOX, l, V_GPF)

            hc4 = hc[:].rearrange("p (c b t) -> p c b t", b=16, t=2)
            S.op("dve", CP(hc4, A[:].rearrange("p (c b t) -> p c b t", b=16, t=128)[:, :, :, 126:128]), reads=RA_all(), writes=[R_hc])
            S.dma("sp", DMA(halo_in_d[:, :], hc[:]), "st_hc", reads=[R_hc], writes=[R_haloin])
            S.dma("pool", CC(halo_in_d[:, :], halo_out_d[:, :]), "cc", reads=[R_haloin], writes=[R_haloout], inc=1)
            S.dma("sp", DMA(hg[:, 0:512], halo_out_d[0:128, :]), "ld_hg", reads=[R_haloout], writes=[R_hg])
            S.dma("sp", DMA(hg[:, 512:1024], halo_out_d[128:256, :]), "ld_hg", reads=[R_haloout], writes=[R_hg], partial=True)
            g0 = hg[:, 0:512].rearrange("p (c b t) -> p c b t", b=16, t=2)
            g1 = hg[:, 512:1024].rearrange("p (c b t) -> p c b t", b=16, t=2)
            hh4 = hh[:].rearrange("p (c b t) -> p c b t", b=16, t=2)
            hh3 = hh[:].rearrange("p (c n) -> p c n", n=32)
            S.op("dve", TS(hh4, g0, cst[:, 6:7], None, ALU.mult), reads=[R_hg, R_const], writes=[R_hh])
            S.op("dve", STT(hh4[:, :, 1:16, :], g1[:, :, 0:15, :], cst[:, 5:6], hh4[:, :, 1:16, :], ALU.mult, ALU.add), reads=[R_hg, R_hh, R_const], writes=[R_hh])

            first_b = True
            for si, (p0, p1) in enumerate(FFN_SPLITS):
                npair = p1 - p0
                uev = [wk[i][:, 0:520].rearrange("p (b t) -> p b t", t=130) for i in range(NWK)]
                for blk0 in range(0, npair, 1):
                    nb2 = 1
                    i, wv = load_w(w_up[l], DC, (p0 + blk0) * 256, nb2 * 256)
                    for pj in range(nb2):
                        j = p0 + blk0 + pj
                        hb_ = 6
                        for half in range(2):
                            for kc in range(DC):
                                S.op("pe", MM(ps[hb_][:, half * 32:half * 32 + 32], wv[:, kc, (pj * 2 + half) * 128:(pj * 2 + half + 1) * 128], hh3[:, kc, :], kc == 0, kc == DC - 1),
                                     reads=[R_WB[i], R_hh] if kc in (0, DC - 1) else (), writes=[R_ps[hb_]])
                        S.op("act", ACT(wk[6][:, 0:64], ps[hb_][:, 0:64], AF.Copy), reads=[R_ps[hb_]], writes=[R_wk[6]])
                        for tt in range(NT):
                            tsl = slice(tt * TW, (tt + 1) * TW)
                            cres = []
                            for half in range(2):
                                ch = j + 44 * half
                                b = (tt * 2 + half) % 4
                                for kc in range(DC):
                                    S.op("pe", MM(ps[b][:], wv[:, kc, (pj * 2 + half) * 128:(pj * 2 + half + 1) * 128], A3[:, kc, tsl], kc == 0, kc == DC - 1),
                                         reads=[R_WB[i]] + RA_tt(tt) if kc in (0, DC - 1) else (), writes=[R_ps[b]])
                                u = half * 2 + (tt % 2)
                                S.op("act", ACT(uev[u][:, :, 2:130], ps[b][:].rearrange("p (b t) -> p b t", t=128), AF.Copy), reads=[R_ps[b]], writes=[R_wk[u]])
                                S.op("act", ACT(uev[u][:, :, 0:2], wk[6][:, half * 32 + tt * 8: half * 32 + tt * 8 + 8].rearrange("p (b t) -> p b t", t=2), AF.Copy),
                                     reads=[R_wk[6], R_wk[u]], writes=[R_wk[u]])
                                cw = 4 + half
                                cv = wk[cw][:, 0:TW].rearrange("p (b t) -> p b t", t=128)
                                S.op("dve", TS(cv, uev[u][:, :, 0:128], V(l, V_CW0, ch), V(l, V_CB, ch), ALU.mult, ALU.add), reads=[R_wk[u], R_const], writes=[R_wk[cw]])
                                S.op("dve", STT(cv, uev[u][:, :, 1:129], V(l, V_CW1, ch), cv, ALU.mult, ALU.add), reads=[R_wk[u], R_wk[cw], R_const], writes=[R_wk[cw]])
                                S.op("dve", STT(cv, uev[u][:, :, 2:130], V(l, V_CW2, ch), cv, ALU.mult, ALU.add), reads=[R_wk[u], R_wk[cw], R_const], writes=[R_wk[cw]])
                            S.op("act", ACT(wk[4][:, 0:TW], wk[4][:, 0:TW], AF.Silu), reads=[R_wk[4]], writes=[R_wk[4]])
                            jl = j - p0
                            S.op("dve", TT(Bb3[:, jl, tsl], wk[4][:, 0:TW], wk[5][:, 0:TW], ALU.mult), reads=[R_wk[4], R_wk[5]],
                                 writes=[R_B[jl][tt]] + (RB_all() if first_b else []))
                            first_b = False
                def consume_down(oc, tt, b, deferred, si=si):
                    w = nxt("wk", 4)
                    if si == 0:
                        S.op("act", ACT(wk[w][:, 0:TW], ps[b][:], AF.Copy), reads=[R_ps[b]], writes=[R_wk[w]])
                    else:
                        S.dma("sp", DMA(wk[w][:, 0:TW], yT_d[:, oc, tt * TW:(tt + 1) * TW]), f"ld_wk{w}", reads=[R_yT[oc][tt]], writes=[R_wk[w]])
                        S.op("dve", TT(wk[w][:, 0:TW], ps[b][:], wk[w][:, 0:TW], ALU.add), reads=[R_ps[b], R_wk[w]], writes=[R_wk[w]])
                    S.dma("sp", DMA(yT_d[:, oc, tt * TW:(tt + 1) * TW], wk[w][:, 0:TW]), f"st_wk{w}", reads=[R_wk[w]], writes=[R_yT[oc][tt]])
                act_res = lambda tt, npair=npair: [R_B[c][tt] for c in range(npair)]
                gemm_fm(w_dn[l], npair, 0, DC, B_ap, act_res, consume_down, [0, 1, 2, 3, 4, 5, 6, 7], row0=p0 * 128)
            if l < L - 1 or DBG:
                nr_pass(l, "mid", V_GPOF, l + 1, V_GPM)
            else:
                nr_pass(l, "final", V_GPOF, l, V_GPM)

        S._waits("sp", [R_out, R_dbg], [R_out, R_dbg])

        sems_e = {e: es.enter_context(nc.semaphore(f"s_{e}")) for e in ENGS}
        sems_d = {s: es.enter_context(nc.semaphore(f"d_{s}")) for s in S.dma_sems}
        block = es.enter_context(nc.Block())
        S.emit(block, sems_e, sems_d)
    return nc


def _vec_cols(v):
    return np.ascontiguousarray(v.reshape(-1, 128).T)


def kernel(x, mem, positions, g_pre_mix, w_in, b_gate, lam_q1, lam_k1, lam_q2, lam_k2,
           g_diff_sub, g_cq, w_uq, g_ckv, w_ukv, w_br_diff, w_br_mla, w_mix_out,
           g_post_mix, g_pre_x, g_mem, w_q_x, w_kv_x, w_o_x, g_post_x, g_pre_ffn,
           w_up, conv_w, conv_b, w_down, g_post_ffn):
    f32 = np.float32
    x = np.asarray(x, f32)
    mem = np.asarray(mem, f32)
    positions = np.asarray(positions, np.int32)
    A_ = lambda a: np.asarray(a, f32)
    w_in, w_uq, w_ukv, w_up = A_(w_in), A_(w_uq), A_(w_ukv), A_(w_up)
    idx_in = np.concatenate([np.arange(1024, 2048), np.arange(2048, 3072), np.arange(3584, 3840), np.arange(3840, 3904),
                             np.arange(3840, 3904), np.arange(0, 1024), np.arange(3072, 3584), np.arange(3904, 8000)])
    w_in_p = np.ascontiguousarray(w_in[:, :, idx_in])
    idx_uq = np.concatenate([np.concatenate([np.arange(h * 192, h * 192 + 128) for h in range(8)]),
                             np.concatenate([np.arange(h * 192 + 128, h * 192 + 192) for h in range(8)])])
    w_uq_p = np.ascontiguousarray(w_uq[:, :, idx_uq])
    idx_ukv = np.concatenate([np.concatenate([np.arange(h * 256, h * 256 + 128) for h in range(8)]),
                              np.concatenate([np.arange(h * 256 + 128, h * 256 + 256) for h in range(8)])])
    w_ukv_p = np.ascontiguousarray(w_ukv[:, :, idx_ukv])
    idx_up = np.concatenate([np.concatenate([np.arange(j * 128, j * 128 + 128), np.arange(5632 + j * 128, 5632 + j * 128 + 128)]) for j in range(44)])
    w_up_p = np.ascontiguousarray(w_up[:, :, idx_up])

    vecs = np.zeros((128, L * NVL), f32)
    cw = A_(conv_w)
    for l in range(L):
        b = l * NVL
        for off, v in ((V_GPM, g_pre_mix), (V_GPOM, g_post_mix), (V_GPX, g_pre_x), (V_GPOX, g_post_x), (V_GPF, g_pre_ffn),
                       (V_GPOF, g_post_ffn), (V_GMEM, g_mem), (V_BG, b_gate), (V_GCQ, g_cq), (V_GCKV, g_ckv), (V_GSUB, g_diff_sub),
                       (V_CB, conv_b)):
            c = _vec_cols(A_(v)[l])
            vecs[:, b + off:b + off + c.shape[1]] = c
        for k, off in enumerate((V_CW0, V_CW1, V_CW2)):
            c = _vec_cols(cw[l, k])
            vecs[:, b + off:b + off + 88] = c
    lamv = np.concatenate([np.concatenate([A_(lam_q1)[l], A_(lam_k1)[l], A_(lam_q2)[l], A_(lam_k2)[l]]) for l in range(L)])[None, :].astype(f32)

    inv = (10000.0 ** (-np.arange(0, 64, 2, dtype=np.float32) / 64.0)).astype(f32)
    mats = np.zeros((128, 384), f32)
    mats[:, 0:128] = np.eye(128, dtype=f32)
    for i in range(128):
        j = (i % 64 + 32) % 64 + (i // 64) * 64
        mats[j, 128 + i] = 1.0
    mats[:, 256:384] = 1.0

    in_maps = []
    for c in range(8):
        b, r = c // 2, c % 2
        xs = np.ascontiguousarray(x[b].reshape(32, 128, D)[r::2].reshape(T, D))
        ps_ = np.ascontiguousarray(positions[b].reshape(32, 128)[r::2].reshape(1, T))
        cst = np.zeros((128, 8), f32)
        pidx = np.arange(128)
        cst[:, 0] = inv[pidx % 32]
        cst[:, 1] = np.where((pidx % 64) < 32, -1.0, 1.0)
        cst[:, 2] = -1.0
        cst[:, 3] = -math.pi
        cst[:, 4] = EPS
        cst[:, 5] = 1.0 if r == 0 else 0.0
        cst[:, 6] = 1.0 if r == 1 else 0.0
        masks = np.zeros((128, 8, TW), f32)
        kp = np.arange(128)
        qq = np.arange(TW)
        for rk in range(2):
            for jj in range(4):
                kchunk = 2 * (2 * jj + rk) + (kp >= 64).astype(np.int64)
                qchunk = 2 * (2 * (qq // 128) + r) + ((qq % 128) >= 64).astype(np.int64)
                vis = kchunk[:, None] <= qchunk[None, :]
                masks[:, rk * 4 + jj, :] = np.where(vis, 0.0, NEG)
        in_maps.append({
            "x": xs, "mem": np.ascontiguousarray(mem[b]), "pos": ps_, "vecs": vecs, "lamv": lamv, "cst": cst,
            "masks": np.ascontiguousarray(masks.reshape(128, 8 * TW)), "mats": mats,
            "w_in": w_in_p, "w_uq": w_uq_p, "w_ukv": w_ukv_p, "w_br_diff": A_(w_br_diff), "w_br_mla": A_(w_br_mla),
            "w_mix_out": A_(w_mix_out), "w_q_x": A_(w_q_x), "w_kv_x": A_(w_kv_x), "w_o_x": A_(w_o_x),
            "w_up": w_up_p, "w_down": A_(w_down),
        })
    nc = build_program()
    res = run_bass_kernel_spmd(nc, in_maps, core_ids=list(range(8)))
    if _DBG["on"]:
        _DBG["data"] = [np.asarray(res.results[c]["dbg"]) for c in range(8)]
        _DBG["dataA"] = [np.asarray(res.results[c]["dbgA"]).astype(np.float32) for c in range(2)]
    out = np.zeros((4, 4096, D), f32)
    for c in range(8):
        b, r = c // 2, c % 2
        out[b].reshape(32, 128, D)[r::2] = np.asarray(res.results[c]["out"], f32).reshape(16, 128, D)
    return out
```
